# Optimizing a Trainium2 kernel written in Bass

```python
import jax, jax.numpy as jnp
from jax import lax
import numpy as np

D_MODEL = 2048
BATCH = 4
SEQ = 2048
DEPTH = 2

CHUNK = 64
N_MIXERS = 2
EPS = 1e-6

GLA_HEADS = 4
GLA_DK = D_MODEL // 2
GLA_DV = D_MODEL
GLA_DK_HEAD = GLA_DK // GLA_HEADS
GLA_DV_HEAD = GLA_DV // GLA_HEADS
GLA_GATE_RANK = 16
GLA_TAU = 16.0
GLA_IN = 2 * GLA_DK + 2 * GLA_DV + GLA_GATE_RANK

SGU_WIDTH = D_MODEL
SGU_BLOCK = 128
SGU_GROUPS = 8
SGU_GROUP_DIM = SGU_WIDTH // SGU_GROUPS
SGU_IN = 3 * SGU_WIDTH

N_GLA_LAYERS = (DEPTH + 1) // 2
N_SGU_LAYERS = DEPTH // 2

kernel_name = "hybrid_gla_sgu_sandwich_trunk"


def rmsnorm(x, gain):
    xf = x.astype(jnp.float32)
    y = xf * lax.rsqrt(jnp.mean(xf * xf, axis=-1, keepdims=True) + EPS)
    return (y * gain.astype(jnp.float32)).astype(x.dtype)


def gla_mixer(h, w_in, w_gate2, b_gate, o_gain, w_out):
    B, S, _ = h.shape
    nc = S // CHUNK
    proj = h @ w_in
    q, k, v, g, glr = jnp.split(
        proj, [GLA_DK, 2 * GLA_DK, 2 * GLA_DK + GLA_DV, 2 * GLA_DK + 2 * GLA_DV], axis=-1)
    log_a = jax.nn.log_sigmoid((glr @ w_gate2 + b_gate).astype(jnp.float32)) / GLA_TAU

    def to_chunks(t, dh):
        return t.astype(jnp.float32).reshape(B, nc, CHUNK, GLA_HEADS, dh).transpose(1, 0, 3, 2, 4)

    qc = to_chunks(q, GLA_DK_HEAD) * (GLA_DK_HEAD ** -0.5)
    kc = to_chunks(k, GLA_DK_HEAD)
    vc = to_chunks(v, GLA_DV_HEAD)
    la = to_chunks(log_a, GLA_DK_HEAD)
    bcum = jnp.cumsum(la, axis=3)
    b_end = bcum[:, :, :, -1:, :]
    k_dec = kc * jnp.exp(b_end - bcum)
    decay = jnp.exp(b_end[:, :, :, 0, :])

    def step(state, xs):
        q_i, k_i, v_i, d_i = xs
        state = state * d_i[..., None] + jnp.einsum('bhck,bhcv->bhkv', k_i, v_i)
        return state, jnp.einsum('bhck,bhkv->bhcv', q_i, state)

    s0 = jnp.zeros((B, GLA_HEADS, GLA_DK_HEAD, GLA_DV_HEAD), jnp.float32)
    _, o = lax.scan(step, s0, (qc, k_dec, vc, decay))
    o = o.transpose(1, 0, 3, 2, 4).reshape(B, S, GLA_HEADS, GLA_DV_HEAD)
    o = o * lax.rsqrt(jnp.mean(o * o, axis=-1, keepdims=True) + EPS)
    o = o.reshape(B, S, GLA_DV) * o_gain.astype(jnp.float32)
    o = o.astype(h.dtype) * jax.nn.silu(g)
    return o @ w_out


def sgu_mixer(h, w_in, ln_gain, ln_bias, w_spatial, b_spatial, w_out):
    B, S, _ = h.shape
    nb = S // SGU_BLOCK
    proj = h @ w_in
    u, v, g = jnp.split(proj, 3, axis=-1)
    u = jax.nn.gelu(u)
    vf = jax.nn.gelu(v).astype(jnp.float32)
    mu = jnp.mean(vf, axis=-1, keepdims=True)
    var = jnp.mean(jnp.square(vf - mu), axis=-1, keepdims=True)
    vn = (vf - mu) * lax.rsqrt(var + EPS) * ln_gain.astype(jnp.float32) + ln_bias.astype(jnp.float32)
    vn = vn.reshape(B, nb, SGU_BLOCK, SGU_GROUPS, SGU_GROUP_DIM)
    pos_chunk = jnp.arange(SGU_BLOCK) // CHUNK
    mask = pos_chunk[:, None] >= pos_chunk[None, :]
    ws = jnp.where(mask[None], w_spatial, 0).astype(jnp.float32)
    vs = jnp.einsum('gij,bnjgd->bnigd', ws, vn) \
        + b_spatial.astype(jnp.float32).T[None, None, :, :, None]
    vs = vs.reshape(B, S, SGU_WIDTH).astype(h.dtype)
    return (u * vs * jax.nn.silu(g)) @ w_out


def setup_inputs(seed: int = 0) -> dict:
    key = jax.random.key(seed)
    ks = jax.random.split(key, 15)
    f32 = jnp.float32
    nrm = lambda k, shape, scale: jax.random.normal(k, shape, f32) * scale
    return {
        "x": nrm(ks[0], (BATCH, SEQ, D_MODEL), 1.0),
        "norm_pre": 1.0 + nrm(ks[1], (DEPTH, D_MODEL), 0.02),
        "norm_post": 1.0 + nrm(ks[2], (DEPTH, D_MODEL), 0.02),
        "gla_w_in": nrm(ks[3], (N_GLA_LAYERS, D_MODEL, GLA_IN), D_MODEL ** -0.5),
        "gla_w_gate2": nrm(ks[4], (N_GLA_LAYERS, GLA_GATE_RANK, GLA_DK), GLA_GATE_RANK ** -0.5),
        "gla_b_gate": nrm(ks[5], (N_GLA_LAYERS, GLA_DK), 0.1),
        "gla_o_gain": 1.0 + nrm(ks[6], (N_GLA_LAYERS, GLA_DV), 0.02),
        "gla_w_out": nrm(ks[7], (N_GLA_LAYERS, GLA_DV, D_MODEL), GLA_DV ** -0.5),
        "sgu_w_in": nrm(ks[8], (N_SGU_LAYERS, D_MODEL, SGU_IN), D_MODEL ** -0.5),
        "sgu_ln_gain": 1.0 + nrm(ks[9], (N_SGU_LAYERS, SGU_WIDTH), 0.02),
        "sgu_ln_bias": nrm(ks[10], (N_SGU_LAYERS, SGU_WIDTH), 0.02),
        "sgu_w_spatial": nrm(ks[11], (N_SGU_LAYERS, SGU_GROUPS, SGU_BLOCK, SGU_BLOCK), SGU_BLOCK ** -0.5),
        "sgu_b_spatial": 1.0 + nrm(ks[12], (N_SGU_LAYERS, SGU_GROUPS, SGU_BLOCK), 0.1),
        "sgu_w_out": nrm(ks[13], (N_SGU_LAYERS, SGU_WIDTH, D_MODEL), SGU_WIDTH ** -0.5),
    }


def reference(x, norm_pre, norm_post, gla_w_in, gla_w_gate2, gla_b_gate, gla_o_gain,
              gla_w_out, sgu_w_in, sgu_ln_gain, sgu_ln_bias, sgu_w_spatial, sgu_b_spatial,
              sgu_w_out):
    for i in range(DEPTH):
        h = rmsnorm(x, norm_pre[i])
        j = i // N_MIXERS
        if i % N_MIXERS == 0:
            y = gla_mixer(h, gla_w_in[j], gla_w_gate2[j], gla_b_gate[j], gla_o_gain[j], gla_w_out[j])
        else:
            y = sgu_mixer(h, sgu_w_in[j], sgu_ln_gain[j], sgu_ln_bias[j], sgu_w_spatial[j],
                          sgu_b_spatial[j], sgu_w_out[j])
        x = x + rmsnorm(y, norm_post[i])
    return x
```

```python
from contextlib import ExitStack
import numpy as np
import concourse.bass as bass
import concourse.mybir as mybir
from concourse.bass_utils import run_bass_kernel_spmd

F32 = mybir.dt.float32
BF16 = mybir.dt.bfloat16
U8 = mybir.dt.uint8
AF = mybir.ActivationFunctionType
ALU = mybir.AluOpType

EPS = 1e-6
KB = 1024
CUT = [0]
DBG = {}


class _Cut(Exception):
    pass


def cut(n):
    if CUT[0] == n:
        raise _Cut()


class Rec:
    __slots__ = ("eng", "fn", "deps", "sig", "semkey", "cnt", "dma")

    def __init__(self, eng, fn, deps, semkey, dma):
        self.eng = eng
        self.fn = fn
        self.deps = deps
        self.sig = False
        self.semkey = semkey
        self.cnt = 0
        self.dma = dma


class Res:
    __slots__ = ("w", "r")

    def __init__(self):
        self.w = None
        self.r = {}


ENGS = ("pe", "act", "dve", "pool", "sp")


class Sched:
    def __init__(self, nc):
        self.nc = nc
        self.q = {e: [] for e in ENGS}

    def op(self, eng, fn, R=(), W=(), deps=(), key=None):
        d = [x for x in deps if x is not None]
        for res in R:
            if res.w is not None:
                d.append(res.w)
        for res in W:
            if res.w is not None:
                d.append(res.w)
            d.extend(res.r.values())
        semkey = eng if key is None else ("dma", key)
        rec = Rec(eng, fn, d, semkey, key is not None)
        self.q[eng].append(rec)
        for res in R:
            res.r[semkey] = rec
        for res in W:
            res.w = rec
            res.r = {}
        return rec

    def emit(self):
        nc = self.nc
        for e in ENGS:
            for r in self.q[e]:
                if r.dma:
                    r.sig = True
                for d in r.deps:
                    d.sig = True
        counts = {}
        for e in ENGS:
            for r in self.q[e]:
                if r.sig:
                    c = counts.get(r.semkey, 0) + (16 if r.dma else 1)
                    counts[r.semkey] = c
                    r.cnt = c
        keys = list(counts.keys())
        with ExitStack() as es:
            sems = {}
            for i, k in enumerate(keys):
                sems[k] = es.enter_context(nc.semaphore("s%d" % i))
            block = es.enter_context(nc.Block())

            def run(ename):
                def body(eng):
                    waited = {}
                    for r in self.q[ename]:
                        need = {}
                        for d in r.deps:
                            if d.cnt > need.get(d.semkey, 0):
                                need[d.semkey] = d.cnt
                        for k, c in need.items():
                            if c > waited.get(k, 0):
                                eng.wait_ge(sems[k], c)
                                waited[k] = c
                        if r.fn is None:
                            continue
                        ins = r.fn(eng)
                        if r.sig:
                            ins.then_inc(sems[r.semkey], 16 if r.dma else 1)
                return body

            block.tensor(run("pe"))
            block.scalar(run("act"))
            block.vector(run("dve"))
            block.gpsimd(run("pool"))
            block.sync(run("sp"))
        return len(keys)


def build(mode="fused"):
    nc = bass.Bass("TRN2", target_bir_lowering=False)
    S = Sched(nc)
    do0 = mode in ("l0", "fused")
    do1 = mode in ("l1", "fused")

    def dram(name, shape, kind="ExternalInput"):
        return nc.dram_tensor(name, list(shape), F32, kind=kind).ap()

    x_d = dram("x", [1024, 2048])
    ident_d = dram("ident", [128, 128])
    gpre_d = dram("gpre", [128, 32])
    npost_d = dram("npost", [2, 128, 2048])
    if do0:
        xp_d = dram("xp", [1024, 2048])
        win0_d = dram("win0", [2048, 6160])
        waug_d = dram("waug", [17, 1024])
        wout0_d = dram("wout0", [2048, 2048])
        ogain_d = dram("ogain", [128, 16])
        umat_d = dram("umat", [128, 128])
        ind_d = dram("ind", [128, 2])
        umatp_d = dram("umatp", [128, 128])
        indp_d = dram("indp", [128, 2])
    if do1:
        win1_d = dram("win1", [2048, 6144])
        wout1_d = dram("wout1", [2048, 2048])
        lng_d = dram("lng", [128, 2048])
        lnb_d = dram("lnb", [128, 2048])
        wsT_d = dram("wsT", [128, 1024])
        bsp_d = dram("bsp", [128, 8])
    out_d = dram("out", [1024, 2048], kind="ExternalOutput")

    A_OFF = 40 * KB
    base = (nc.sbuf_base + 31) // 32 * 32
    ARENA = 206 * KB
    nc.alloc_sbuf_tensor("arena", [128, ARENA], U8)
    _cnt = [0]

    def T(shape, dt, off):
        _cnt[0] += 1
        assert off % 32 == 0
        esz = 4 if dt == F32 else 2
        n = 1
        for s in shape[1:]:
            n *= s
        assert off + n * esz <= ARENA, (shape, off)
        return nc.alloc_sbuf_tensor_at("t%d" % _cnt[0], list(shape), dt, offset=base + off)

    ident = T([128, 128], F32, 0)
    umat = T([128, 128], F32, 512)
    ind = T([128, 2], F32, 1024)
    gpre = T([128, 2, 16], F32, 1056)
    ogain = T([128, 16], F32, 1184)
    bsp = T([128, 8], F32, 1248)
    stat = T([128, 256], F32, 1280)
    bnst = T([128, 8, 4, 6], F32, 2304)
    gaug = T([128, 1024], BF16, 3328)
    waug = T([128, 1024], BF16, 5376)
    decay = T([128, 8, 8, 2], F32, 7424)
    neghalf = T([128, 16], F32, 7936)
    epsc = T([128, 8], F32, 8000)
    wsT = T([128, 8, 128], BF16, 3328)
    mv = T([128, 8, 2], F32, 5376)
    lnsc = T([128, 8, 2], F32, 5440)
    wbufs = [T([128, 16, 512], BF16, 8 * KB + i * 16 * KB) for i in range(2)]
    wres = [Res(), Res()]
    A = 40 * KB

    es = ExitStack()
    banks = [es.enter_context(nc.psum_tensor("pb%d" % i, [128, 512], F32)) for i in range(8)]
    bres = [Res() for _ in range(8)]
    PJ = [0, 1, 2]
    TRB = [3, 4]
    UPB = [5, 6]
    OBK = 7
    pj_i = [0]
    tr_i = [0]

    def next_pj():
        b = PJ[pj_i[0] % len(PJ)]
        pj_i[0] += 1
        return b

    def next_tr():
        b = TRB[tr_i[0] % len(TRB)]
        tr_i[0] += 1
        return b

    st_i = [1]

    def stat_slot(n=1):
        i = st_i[0]
        assert i + n <= 256, "stat slots exhausted"
        st_i[0] = i + n
        return i

    cres = Res()

    def ld(dst, src, key):
        S.op("sp", lambda e: e.dma_start(out=dst, in_=src), W=[cres], key=key)

    ld(ident[:], ident_d[:, :], "c0")
    ld(gpre[:].rearrange("p a b -> p (a b)"), gpre_d[:, :], "c1")
    if do0:
        ld(umat[:], umat_d[:, :], "c2")
        ld(ind[:], ind_d[:, :], "c3")
        umatP = T([128, 128], F32, A_OFF + 160 * KB)
        indP = T([128, 2], F32, 8032)
        ld(umatP[:], umatp_d[:, :], "c5")
        ld(indP[:], indp_d[:, :], "c6")
        ld(ogain[:], ogain_d[:, :], "c4")
    S.op("dve", lambda e: e.memset(epsc[:], EPS))
    cbar = S.op("dve", lambda e: e.memset(neghalf[:], -0.5), R=[], W=[],
                deps=[r for r in S.q["sp"]])
    constres = Res()
    constres.w = cbar

    wq = {"i": 0}

    def wload(src2d, ncols, pad=None):
        i = wq["i"] % 2
        wq["i"] += 1
        wb = wbufs[i]
        src = src2d.rearrange("(k p) n -> p k n", p=128)
        S.op("pool", lambda e: e.dma_start(out=wb[:, :, 0:ncols], in_=src), W=[wres[i]], key="w%d" % i)
        if pad is not None:
            psrc, pn = pad
            psrc = psrc.rearrange("(k p) n -> p k n", p=128)
            rec = S.op("pool", lambda e: e.dma_start(out=wb[:, :, ncols:ncols + pn], in_=psrc), key="w%d" % i)
            wres[i].w = rec
        return wb, wres[i]

    class WStream:
        def __init__(self):
            self.plan = []
            self.issued = []

        def add(self, src2d, ncols, pad=None):
            self.plan.append((src2d, ncols, pad))
            return len(self.plan) - 1

        def warm(self):
            while len(self.issued) < min(2, len(self.plan)):
                s_, n_, pad_ = self.plan[len(self.issued)]
                self.issued.append(wload(s_, n_, pad_))

        def get(self, j):
            assert j == getattr(self, "last", -1) + 1, "weight chunks must be consumed in plan order"
            self.last = j
            while len(self.issued) <= min(j + 1, len(self.plan) - 1):
                s, n, pad = self.plan[len(self.issued)]
                self.issued.append(wload(s, n, pad))
            return self.issued[j]

    WS = WStream()

    def rstd_from_ss(ss_ap, n_elems, out_ap, R, W):
        S.op("act", lambda e: e.activation(out=out_ap, in_=ss_ap, func=AF.Ln, scale=1.0 / n_elems, bias=epsc[:, 0:1]), R=R + [constres], W=W)
        S.op("act", lambda e: e.activation(out=out_ap, in_=out_ap, func=AF.Exp, scale=-0.5), R=W, W=W)

    def prenorm_tile(layer, t, src_ap, src_res, from_dram, XT, xtres, junk, junkres, hT, hres, i, dma_extra_w=(), pool_rstd=False,
                     part="both"):
        xt = XT[i]
        if part == "back":
            return prenorm_back(layer, t, xt, xtres, hT, hres, i)
        if from_dram:
            S.op("sp", lambda e: e.dma_start(out=xt[:], in_=src_ap), W=[xtres[i]] + list(dma_extra_w), key="xt%d" % i)
            xin, xin_res = xt[:], xtres[i]
        else:
            xin, xin_res = src_ap, src_res
        sl = stat_slot(2)
        sres = Res()
        S.op("act", lambda e: e.activation(out=junk[:], in_=xin, func=AF.Square, accum_out=stat[:, sl:sl + 1]),
             R=[xin_res, constres], W=(list(junkres) if isinstance(junkres, (list, tuple)) else [junkres]) + [sres])
        if pool_rstd:
            S.op("pool", lambda e: e.tensor_scalar(out=stat[:, sl + 1:sl + 2], in0=stat[:, sl:sl + 1], scalar1=1.0 / 2048.0, scalar2=EPS,
                                                   op0=ALU.mult, op1=ALU.add), R=[sres], W=[sres])
            S.op("pool", lambda e: e.tensor_tensor(out=stat[:, sl + 1:sl + 2], in0=stat[:, sl + 1:sl + 2], in1=neghalf[:, 0:1], op=ALU.pow),
                 R=[sres, constres], W=[sres])
        else:
            rstd_from_ss(stat[:, sl:sl + 1], 2048.0, stat[:, sl + 1:sl + 2], [sres], [sres])
        S.op("dve", lambda e: e.tensor_scalar(out=xt[:], in0=xin, scalar1=stat[:, sl + 1:sl + 2], scalar2=None,
                                              op0=ALU.mult), R=[xin_res, sres], W=[xtres[i]])
        if part == "front":
            return
        prenorm_back(layer, t, xt, xtres, hT, hres, i)

    def prenorm_back(layer, t, xt, xtres, hT, hres, i):
        for b in range(4):
            tb = next_tr()

            def tfn(e, b=b, tb=tb):
                for j in range(4):
                    ins = e.transpose(out=banks[tb][:, j * 128:(j + 1) * 128],
                                      in_=xt[:, (4 * b + j) * 128:(4 * b + j + 1) * 128], identity=ident[:])
                return ins
            S.op("pe", tfn, R=[xtres[i], constres], W=[bres[tb]])
            S.op("dve", lambda e, b=b, tb=tb: e.tensor_tensor(
                out=hT[:, 4 * b:4 * b + 4, t * 128:(t + 1) * 128],
                in0=banks[tb][:].rearrange("p (j c) -> p j c", j=4),
                in1=gpre[:, layer, 4 * b:4 * b + 4].unsqueeze(2).to_broadcast([128, 4, 128]),
                op=ALU.mult), R=[bres[tb], constres], W=[hres[t][b]])

    def proj_tok(wb, wr, ncols, hT, hres_t, t):
        pb = next_pj()

        def fn(e):
            for k in range(16):
                ins = e.matmul(banks[pb][:, 0:ncols], lhsT=hT[:, k, t * 128:(t + 1) * 128], rhs=wb[:, k, 0:ncols],
                               start=(k == 0), stop=(k == 15))
            return ins
        S.op("pe", fn, R=[wr] + hres_t, W=[bres[pb]])
        return pb

    def outproj_plan(wout_d, ngrp=2):
        ids = {}
        for grp in range(ngrp):
            for n in range(4):
                ids[(grp, n)] = WS.add(wout_d[:, n * 512:(n + 1) * 512], 512)
        return ids

    def l1_plan():
        CU, CVV, CGG = 0, 2048, 4096
        id_v1 = [WS.add(win1_d[:, CVV + n * 512:CVV + (n + 1) * 512], 512) for n in range(4)]
        id_u1 = [WS.add(win1_d[:, CU + n * 512:CU + (n + 1) * 512], 512) for n in range(4)]
        id_g1 = [WS.add(win1_d[:, CGG + n * 512:CGG + (n + 1) * 512], 512) for n in range(4)]
        return id_v1, id_u1, id_g1, outproj_plan(wout1_d, 1)

    def outproj(layer, oT_lhsT, oT_res, ids, ybuf, ybres, nchunk, ncres, junk, junkres, finish, ngrp=2, yslice=None):
        TG = 8 // ngrp
        for grp in range(ngrp):
            yss = stat_slot(4 * TG)
            yssres = [Res() for _ in range(TG)]

            def fin(tt, grp=grp, yss=yss, yssres=yssres):
                t = grp * TG + tt
                sl = stat_slot(2)
                r2 = Res()
                S.op("dve", lambda e: e.reduce_sum(out=stat[:, sl:sl + 1], in_=stat[:, yss + tt * 4:yss + tt * 4 + 4],
                                                   axis=mybir.AxisListType.X), R=[yssres[tt]], W=[r2])
                rstd_from_ss(stat[:, sl:sl + 1], 2048.0, stat[:, sl + 1:sl + 2], [r2], [r2])
                finish(t, tt, stat[:, sl + 1:sl + 2], r2)

            for n in range(4):
                wb, wr = WS.get(ids[(grp, n)])
                nci = n % 2
                S.op("sp", lambda e, n=n, nci=nci: e.dma_start(out=nchunk[nci][:], in_=npost_d[layer, :, n * 512:(n + 1) * 512]),
                     W=[ncres[nci]], key="nc%d" % nci)
                for tt in range(TG):
                    t = grp * TG + tt
                    pb = next_pj()

                    def fn(e, t=t, pb=pb, wb=wb):
                        for k in range(16):
                            ins = e.matmul(banks[pb][:], lhsT=oT_lhsT(k, t), rhs=wb[:, k, :], start=(k == 0), stop=(k == 15))
                        return ins
                    S.op("pe", fn, R=[wr] + oT_res(t), W=[bres[pb]])
                    sq = S.op("act", lambda e, pb=pb, tt=tt, n=n, yss=yss: e.activation(out=junk[:, 0:512], in_=banks[pb][:], func=AF.Square,
                                                                            accum_out=stat[:, yss + tt * 4 + n:yss + tt * 4 + n + 1]),
                         R=[bres[pb]], W=[junkres, yssres[tt]])
                    if yslice is None:
                        yo, vw = ybuf[tt][:, n * 512:(n + 1) * 512], (lambda ap: ap)
                    else:
                        yo, vw = yslice(tt, n)
                    S.op("dve", lambda e, pb=pb, nci=nci, yo=yo, vw=vw: e.tensor_tensor(
                        out=yo, in0=vw(banks[pb][:]), in1=vw(nchunk[nci][:]), op=ALU.mult),
                        R=[bres[pb], ncres[nci]], W=[ybres[tt]], deps=[sq])
                    if n == 3 and tt > 0:
                        fin(tt - 1)
            fin(TG - 1)

    x1 = T([128, 8, 2048], F32, A + 100 * KB)
    x1res = [Res() for _ in range(8)]

    try:
        if do0:
            hT = T([128, 16, 1024], BF16, A + 0)
            kexp = T([128, 8, 1024], F32, A + 32 * KB)
            oT = T([128, 16, 1024], BF16, A + 32 * KB)
            kdec = T([128, 8, 1024], BF16, A + 64 * KB)
            vEO = [T([128, 8, 512], BF16, A + 80 * KB + i * 8 * KB) for i in range(2)]
            Sst = T([128, 4, 2, 512], F32, A + 96 * KB)
            Sbf = [T([128, 2, 512], BF16, A + 112 * KB + i * 2 * KB) for i in range(2)]
            qEO = [T([128, 2, 8, 128], BF16, A + 116 * KB + i * 4 * KB) for i in range(2)]
            sg = [T([128, 8, 512], F32, A + 124 * KB + i * 16 * KB) for i in range(2)]
            onb = [T([128, 512], F32, A + 156 * KB + i * 2 * KB) for i in range(2)]
            wglr = T([128, 16, 128], BF16, A + 156 * KB)
            XT = [T([128, 2048], F32, A + 124 * KB + i * 8 * KB) for i in range(2)]
            junk = T([128, 2048], BF16, A + 140 * KB)
            lbuf = T([128, 512], F32, A + 144 * KB)
            lbufs = [lbuf, T([128, 512], F32, A + 148 * KB)]
            lress = [Res(), Res()]
            ebuf = T([128, 512], F32, A + 146 * KB)
            etmp = [T([128, 512], F32, A + 160 * KB + i * 2 * KB) for i in range(2)]

            hres = [[Res() for _ in range(4)] for _ in range(8)]
            xtres = [Res(), Res()]
            junkres = Res()
            kexpres = [[Res(), Res()] for _ in range(8)]
            kdres = [[Res(), Res()] for _ in range(8)]
            vres = [Res() for _ in range(8)]
            sres = [[Res(), Res()] for _ in range(4)]
            sbres = [[Res(), Res()] for _ in range(2)]
            qres = [Res() for _ in range(4)]
            sgres = [[Res() for _ in range(8)] for _ in range(2)]
            onres = [Res(), Res()]
            oTres = [[Res() for _ in range(4)] for _ in range(8)]
            decres = [[Res(), Res()] for _ in range(8)]
            gaugres = [Res(), Res()]
            lres, eres = Res(), Res()
            etres = [Res(), Res()]
            aliasfence = Res()

            CQ, CK, CV, CG, CL = 0, 1024, 2048, 4096, 6144

            def gla_plan(prev):
                id_glr = None
                id_k, id_v, id_q, id_g = [], [], [], []
                for h in range(4):
                    if h == 1:
                        id_k = [WS.add(win0_d[:, CK + n * 512:CK + (n + 1) * 512], 512) for n in range(2)]
                    if not prev and h > 0:
                        id_g.append(WS.add(win0_d[:, CG + h * 512:CG + (h + 1) * 512], 512))
                    id_v.append(WS.add(win0_d[:, CV + h * 512:CV + (h + 1) * 512], 512))
                    if not prev:
                        id_q.append(WS.add(win0_d[:, CQ + h * 256:CQ + (h + 1) * 256], 256))
                    if not prev and h == 0:
                        id_g.append(WS.add(win0_d[:, CG + h * 512:CG + (h + 1) * 512], 512))
                return id_glr, id_k, id_v, id_q, id_g

            gla_plans = {True: gla_plan(True), False: gla_plan(False)}
            out_ids0 = outproj_plan(wout0_d, 1)
            if do1:
                l1_ids = l1_plan()

            WS.warm()
            waugres = Res()
            S.op("dve", lambda e: e.memset(waug[:], 0.0), W=[waugres])
            S.op("pool", lambda e: e.dma_start(out=waug[0:17, :], in_=waug_d[:, :]), W=[waugres], key="waug")
            S.op("pool", lambda e: e.memset(Sst[:].rearrange("p a b c -> p (a b c)"), 0.0), W=[r for hs in sres for r in hs])


            def gla_phase(prev, carry=None):
                src_d = xp_d if prev else x_d
                id_glr, id_k, id_v, id_q, id_g = gla_plans[prev]
                kd_all = [r for t_ in range(8) for r in kdres[t_]]

                def prenorm_gen():
                    def pt(t, part):
                        prenorm_tile(0, t, src_d[t * 128:(t + 1) * 128, :], None, True, XT, xtres, junk, junkres, hT, hres, t % 2,
                                     dma_extra_w=(), part=part)
                    pt(0, "front")
                    for t in range(8):
                        pt(t, "back")
                        if t + 1 < 8:
                            pt(t + 1, "front")
                        yield
                S.op("dve", lambda e: e.memset(gaug[:, :], 0.0), W=gaugres)
                S.op("dve", lambda e: e.memset(gaug[0:32, :], 1.0), W=gaugres)
                wb_glr = wglr
                wr_glr = Res()
                glr_src = win0_d[:, CL:CL + 16].rearrange("(k p) n -> p k n", p=128)
                pad_src = win0_d[:, 0:112].rearrange("(k p) n -> p k n", p=128)
                S.op("pool", lambda e: e.dma_start(out=wglr[:, :, 0:16], in_=glr_src), W=[wr_glr, onres[0], onres[1]], key="wglr")
                S.op("pool", lambda e: e.dma_start(out=wglr[:, :, 16:128], in_=pad_src), W=[wr_glr, onres[0], onres[1]], key="wglr")

                def glr_group(G):
                    pb = next_pj()

                    def fn(e):
                        for k in range(16):
                            ins = e.matmul(banks[pb][:, :], lhsT=wb_glr[:, k, 0:128], rhs=hT[:, k, G * 512:(G + 1) * 512],
                                           start=(k == 0), stop=(k == 15))
                        return ins
                    S.op("pe", fn, R=[wr_glr, onres[0], onres[1]] + [r for t in range(4 * G, 4 * G + 4) for r in hres[t]], W=[bres[pb]])
                    S.op("act", lambda e: e.activation(out=gaug[0:16, G * 512:(G + 1) * 512], in_=banks[pb][0:16, :], func=AF.Copy),
                         R=[bres[pb]], W=[gaugres[G]])
                def gating_gen():
                    GA, GB, GC = UPB[0], UPB[1], OBK

                    def stage_b(t, half, li):
                        S.op("pe", lambda e: e.matmul(banks[GB][:], lhsT=(umatP if prev else umat)[:], rhs=lbufs[li][:], start=True, stop=True),
                             R=[lress[li], constres], W=[bres[GB]])

                        def totfn(e):
                            for j in range(4):
                                ins = e.matmul(banks[GC][:, 2 * j:2 * j + 2], lhsT=lbufs[li][:, j * 128:(j + 1) * 128], rhs=(indP if prev else ind)[:],
                                               start=True, stop=True)
                            return ins
                        S.op("pe", totfn, R=[lress[li], constres], W=[bres[GC]])
                        S.op("act", lambda e: e.activation(out=kexp[:, t, half * 512:(half + 1) * 512], in_=banks[GB][:], func=AF.Exp),
                             R=[bres[GB]], W=[kexpres[t][half]])
                        S.op("act", lambda e: e.activation(
                            out=decay[:, t, half * 4:half * 4 + 4, :], in_=banks[GC][:, 0:8].rearrange("p (j c) -> p j c", j=4), func=AF.Exp),
                            R=[bres[GC]], W=[decres[t][half]])

                    pend = None
                    i = 0
                    for t in range(8):
                        for half in range(2):
                            li = i % 2
                            i += 1
                            S.op("pe", lambda e, t=t, half=half: e.matmul(banks[GA][:], lhsT=gaug[:, t * 128:(t + 1) * 128],
                                                                           rhs=waug[:, half * 512:(half + 1) * 512], start=True, stop=True),
                                 R=[gaugres[t // 4], waugres], W=[bres[GA]])
                            S.op("act", lambda e: e.activation(out=ebuf[:], in_=banks[GA][:], func=AF.Exp, scale=-1.0),
                                 R=[bres[GA], aliasfence], W=[eres])
                            S.op("act", lambda e, li=li: e.activation(out=lbufs[li][:], in_=ebuf[:], func=AF.Ln, bias=1.0),
                                 R=[eres, aliasfence], W=[lress[li]])
                            if pend is not None:
                                stage_b(*pend)
                            pend = (t, half, li)
                            yield
                    stage_b(*pend)
                    yield

                cut(4)
                def stream_inproj(h):
                    hb = h % 2

                    def part_g():
                        wb, wr = WS.get(id_g[h])
                        for t in range(8):
                            pb = proj_tok(wb, wr, 512, hT, hres[t], t)
                            ei = t % 2
                            if h > 0:
                                S.op("act", lambda e, pb=pb, t=t: e.activation(out=sg[hb][:, t, :], in_=banks[pb][:], func=AF.Silu),
                                     R=[bres[pb]], W=[sgres[hb][t]] + ([aliasfence] if hb == 1 else []))
                                yield
                                continue
                            S.op("act", lambda e, pb=pb, ei=ei: e.activation(out=etmp[ei][:], in_=banks[pb][:], func=AF.Exp, scale=-1.0),
                                 R=[bres[pb]], W=[etres[ei]])
                            S.op("act", lambda e, ei=ei: e.activation(out=etmp[ei][:], in_=etmp[ei][:], func=AF.Ln, bias=1.0),
                                 R=[etres[ei]], W=[etres[ei]])
                            S.op("act", lambda e, ei=ei: e.activation(out=etmp[ei][:], in_=etmp[ei][:], func=AF.Exp, scale=-1.0),
                                 R=[etres[ei]], W=[etres[ei]])
                            S.op("dve", lambda e, pb=pb, t=t, ei=ei: e.tensor_tensor(out=sg[hb][:, t, :], in0=banks[pb][:], in1=etmp[ei][:], op=ALU.mult),
                                 R=[bres[pb], etres[ei]], W=[sgres[hb][t]] + ([aliasfence] if hb == 1 else []))
                            yield

                    def part_v():
                        wb, wr = WS.get(id_v[h])
                        for t in range(8):
                            pb = proj_tok(wb, wr, 512, hT, hres[t], t)
                            if prev:
                                S.op("act", lambda e, pb=pb, t=t: e.activation(out=vEO[0][:, t, :], in_=banks[pb][:], func=AF.Copy),
                                     R=[bres[pb]], W=[vres[t]])
                                yield
                                continue
                            S.op("act", lambda e, pb=pb, t=t: e.activation(out=vEO[0][0:64, t, :], in_=banks[pb][0:64, :], func=AF.Copy),
                                 R=[bres[pb]], W=[vres[t]])
                            S.op("act", lambda e, pb=pb, t=t: e.activation(out=vEO[1][64:128, t, :], in_=banks[pb][64:128, :], func=AF.Copy),
                                 R=[bres[pb]], W=[vres[t]])
                            yield

                    def part_q():
                        wb, wr = WS.get(id_q[h])
                        for s in range(2):
                            for G in range(2):
                                pb = next_pj()

                                def fn(e, s=s, G=G, pb=pb, wb=wb):
                                    for k in range(16):
                                        ins = e.matmul(banks[pb][:], lhsT=wb[:, k, s * 128:(s + 1) * 128], rhs=hT[:, k, G * 512:(G + 1) * 512],
                                                       start=(k == 0), stop=(k == 15))
                                    return ins
                                S.op("pe", fn, R=[wr] + [r for t in range(4 * G, 4 * G + 4) for r in hres[t]], W=[bres[pb]])
                                pv = banks[pb][:].rearrange("p (t c) -> p t c", t=4)
                                S.op("dve", lambda e, s=s, G=G, pv=pv: e.tensor_scalar(out=qEO[0][:, s, 4 * G:4 * G + 4, 0:64], in0=pv[:, :, 0:64],
                                                                                        scalar1=1.0 / 16.0, scalar2=None, op0=ALU.mult),
                                     R=[bres[pb]], W=[qres[s * 2 + G]])
                                S.op("dve", lambda e, s=s, G=G, pv=pv: e.tensor_scalar(out=qEO[1][:, s, 4 * G:4 * G + 4, 64:128], in0=pv[:, :, 64:128],
                                                                                        scalar1=1.0 / 16.0, scalar2=None, op0=ALU.mult),
                                     R=[bres[pb]], W=[qres[s * 2 + G]])
                                yield


                    if prev:
                        yield from part_v()
                    elif h == 0:
                        yield from part_v()
                        yield from part_q()
                        yield from part_g()
                    else:
                        yield from part_g()
                        yield from part_v()
                        yield from part_q()

                def stream_scan(h):
                    hb = h % 2

                    def o_mm(c):
                        t, hh = c // 2, c % 2
                        ci = c % 2

                        def ofn(e):
                            for s_ in range(2):
                                ins = e.matmul(banks[OBK][:], lhsT=qEO[hh][:, s_, t, :], rhs=Sbf[ci][:, s_, :],
                                               start=(hh == 0 and s_ == 0), stop=(hh == 1 and s_ == 1))
                            return ins
                        S.op("pe", ofn, R=[qres[0 + (c // 8)], qres[2 + (c // 8)], sbres[ci][0], sbres[ci][1]], W=[obres])

                    def opost_stats(t):
                        sl = stat_slot(2)
                        r2 = Res()
                        S.op("act", lambda e: e.activation(out=junk2[:], in_=banks[OBK][:], func=AF.Square, accum_out=stat[:, sl:sl + 1]),
                             R=[obres], W=[junk2res, r2])
                        S.op("pool", lambda e: e.tensor_scalar(out=stat[:, sl + 1:sl + 2], in0=stat[:, sl:sl + 1], scalar1=1.0 / 512.0, scalar2=EPS,
                                                               op0=ALU.mult, op1=ALU.add), R=[r2], W=[r2])
                        S.op("pool", lambda e: e.tensor_tensor(out=stat[:, sl + 1:sl + 2], in0=stat[:, sl + 1:sl + 2], in1=neghalf[:, 0:1], op=ALU.pow),
                             R=[r2, constres], W=[r2])
                        return sl, r2

                    def opost_apply(t, sl, r2):
                        oi = t % 2
                        S.op("dve", lambda e: e.scalar_tensor_tensor(
                            out=onb[oi][:], in0=banks[OBK][:], scalar=stat[:, sl + 1:sl + 2], in1=sg[hb][:, t, :],
                            op0=ALU.mult, op1=ALU.mult), R=[obres, r2, sgres[hb][t]], W=[onres[oi]])

                    def opost_tr(t):
                        oi = t % 2
                        tb = next_tr()

                        def tfn(e):
                            for j in range(4):
                                ins = e.transpose(out=banks[tb][:, j * 128:(j + 1) * 128], in_=onb[oi][:, j * 128:(j + 1) * 128],
                                                  identity=ident[:])
                            return ins
                        S.op("pe", tfn, R=[onres[oi], constres], W=[bres[tb]])
                        return tb

                    def opost_evac(t, tb):
                        S.op("dve", lambda e: e.tensor_tensor(
                            out=oT[:, 4 * h:4 * h + 4, t * 128:(t + 1) * 128],
                            in0=banks[tb][:].rearrange("p (j c) -> p j c", j=4),
                            in1=ogain[:, 4 * h:4 * h + 4].unsqueeze(2).to_broadcast([128, 4, 128]),
                            op=ALU.mult), R=[bres[tb], constres], W=[oTres[t][h]] + kexpres_all)

                    pend_o = None
                    pend_tr = None
                    for c in range(17 if not prev else 8):
                        t, hh = (c // 2, c % 2) if not prev else (c, 0)
                        do_upd = c < 16
                        if do_upd:
                            for s in range(2):
                                ub = (TRB[s] if (prev and c % 2 == 1) else UPB[s])
                                S.op("pe", lambda e, t=t, hh=hh, s=s, ub=ub: e.matmul(
                                    banks[ub][:], lhsT=kdec[:, t, h * 256 + s * 128:h * 256 + (s + 1) * 128],
                                    rhs=vEO[hh][:, t, :], start=True, stop=True),
                                    R=[kdres[t][h // 2], vres[t]], W=[bres[ub]])
                        o_now = pend_o
                        if o_now is not None:
                            o_mm(o_now)
                        tb = opost_tr(pend_tr) if pend_tr is not None else None
                        if do_upd:
                            for s in range(2):
                                ub = (TRB[s] if (prev and c % 2 == 1) else UPB[s])
                                S.op("dve", lambda e, t=t, hh=hh, s=s, ub=ub: e.scalar_tensor_tensor(
                                    out=Sst[:, h, s, :], in0=Sst[:, h, s, :], scalar=decay[:, t, h * 2 + s, hh:hh + 1], in1=banks[ub][:],
                                    op0=ALU.mult, op1=ALU.add), R=[bres[ub], decres[t][h // 2]], W=[sres[h][s]])
                        if tb is not None:
                            opost_evac(pend_tr, tb)
                            pend_tr = None
                        stats = None
                        if o_now is not None and o_now % 2 == 1:
                            stats = opost_stats(o_now // 2)
                        if do_upd and not prev:
                            ci = c % 2
                            for s in range(2):
                                S.op("act", lambda e, s=s, ci=ci: e.activation(out=Sbf[ci][:, s, :], in_=Sst[:, h, s, :], func=AF.Copy),
                                     R=[sres[h][s]], W=[sbres[ci][s]])
                        if stats is not None:
                            opost_apply(o_now // 2, *stats)
                            pend_tr = o_now // 2
                        pend_o = c if (do_upd and not prev) else None
                        yield
                    if pend_tr is not None:
                        tb = opost_tr(pend_tr)
                        opost_evac(pend_tr, tb)
                        yield

                def drain(g):
                    for _ in g:
                        pass

                def merge(a, b, na=1, nb=1):
                    done_a = done_b = False
                    while not (done_a and done_b):
                        for _ in range(nb):
                            if not done_b:
                                try:
                                    next(b)
                                except StopIteration:
                                    done_b = True
                        for _ in range(na):
                            if not done_a:
                                try:
                                    next(a)
                                except StopIteration:
                                    done_a = True

                if carry is not None:
                    S.op("pool", lambda e: e.memset(vEO[1][0:64, :, :], 0.0), W=vres)
                    S.op("pool", lambda e: e.memset(qEO[0][:, :, :, 64:128], 0.0), W=qres)
                    S.op("pool", lambda e: e.memset(qEO[1][:, :, :, 0:64], 0.0), W=qres)
                pre = prenorm_gen()
                for _ in range(2):
                    next(pre)
                    if carry is not None:
                        for _c in range(4):
                            next(carry, None)
                if carry is not None:
                    drain(carry)
                    S.op("pool", lambda e: e.memset(vEO[0][64:128, :, :], 0.0), W=vres)
                ga, gb = stream_inproj(0), gating_gen()
                for _ in range(2):
                    next(ga)
                    next(pre)
                glr_group(0)
                for _ in range(4):
                    next(gb)
                    next(gb)
                    next(ga)
                    next(pre)
                glr_group(1)
                merge(ga, gb, 1, 2 if prev else 1)
                for n in range(2):
                    wb, wr = WS.get(id_k[n])
                    for t in range(8):
                        pb = proj_tok(wb, wr, 512, hT, hres[t], t)
                        S.op("dve", lambda e, pb=pb, t=t, n=n: e.tensor_tensor(out=kdec[:, t, n * 512:(n + 1) * 512], in0=banks[pb][:],
                                                                                in1=kexp[:, t, n * 512:(n + 1) * 512], op=ALU.mult),
                             R=[bres[pb], kexpres[t][n]], W=[kdres[t][n]])


                cut(41)
                if CUT[0] == 42:
                    drain(stream_scan(0))
                    cut(42)
                for h in range(4):
                    if h + 1 < 4:
                        merge(stream_inproj(h + 1), stream_scan(h), 1, 1)
                    elif prev:
                        return stream_scan(h)
                    else:
                        drain(stream_scan(h))

            obres = bres[OBK]
            DBG.update(oT=oT.name, kdec=kdec.name, vE=vEO[0].name, vO=vEO[1].name, qE=qEO[0].name, qO=qEO[1].name, Sst=Sst.name, decay=decay.name, gaug=gaug.name, hT=hT.name)
            junk2 = T([128, 512], BF16, 2304)
            junk2res = Res()
            kexpres_all = [r for t in range(8) for r in kexpres[t]]

            cut(1)
            carry0 = gla_phase(True)
            cut(5)
            gla_phase(False, carry0)
            cut(6)

            ybuf = [x1[:, t, :] for t in range(8)]
            ybres = x1res
            nchunk = [T([128, 512], F32, A + 0 + i * 2 * KB) for i in range(2)]
            ncres = [Res(), Res()]
            XR = [T([128, 2048], F32, A + 8 * KB + i * 8 * KB) for i in range(2)]
            xrres = [Res(), Res()]
            junk3 = T([128, 512], BF16, A + 24 * KB)
            junk3res = Res()
            lastrecs = [S.q[e][-1] for e in ENGS if S.q[e]]
            fence0 = S.op("dve", lambda e: e.memset(stat[:, 0:1], 0.0), deps=lastrecs)
            for r in ybres + ncres + xrres + [junk3res] + x1res:
                r.w = fence0

            def finish0(t, tt, rstd_ap, rres):
                i = t % 2
                S.op("sp", lambda e: e.dma_start(out=XR[i][:], in_=x_d[t * 128:(t + 1) * 128, :]), W=[xrres[i]], key="xr%d" % i)
                S.op("dve", lambda e: e.scalar_tensor_tensor(out=x1[:, t, :], in0=x1[:, t, :], scalar=rstd_ap, in1=XR[i][:],
                                                             op0=ALU.mult, op1=ALU.add),
                     R=[ybres[tt], rres, xrres[i]], W=[x1res[t]])
                if mode == "l0":
                    S.op("sp", lambda e: e.dma_start(out=out_d[t * 128:(t + 1) * 128, :], in_=x1[:, t, :]), R=[x1res[t]], key="o%d" % t)

            outproj(0, lambda k, t: oT[:, k, t * 128:(t + 1) * 128], lambda t: oTres[t], out_ids0, ybuf, ybres, nchunk, ncres,
                    junk3, junk3res, finish0, ngrp=1)

        if do1:
            lastrecs = [S.q[e][-1] for e in ENGS if S.q[e]]
            fence1 = S.op("dve", lambda e: e.memset(stat[:, 0:1], 0.0), deps=lastrecs)
            hT1 = T([128, 16, 1024], BF16, A + 0)
            P = T([128, 8, 4, 512], F32, A + 32 * KB)
            Pbf = P.bitcast(BF16) if hasattr(P, "bitcast") else None
            tmpA = [T([128, 512], F32, A + 96 * KB + i * 2 * KB) for i in range(2)]
            XT1 = [T([128, 2048], F32, A + 32 * KB + (6 + i) * 8 * KB) for i in range(2)]
            junk1 = T([128, 2048], BF16, A + 96 * KB)
            hres1 = [[Res() for _ in range(4)] for _ in range(8)]
            xt1res = [Res(), Res()]
            junk1res = Res()
            pres = [[Res() for _ in range(4)] for _ in range(8)]
            tmpres = [Res(), Res()]
            for r in [x for hs in hres1 for x in hs] + xt1res + [junk1res] + tmpres:
                r.w = fence1
            if mode == "l1":
                for t in range(8):
                    S.op("sp", lambda e, t=t: e.dma_start(out=x1[:, t, :], in_=x_d[t * 128:(t + 1) * 128, :]), W=[x1res[t]], key="x1l%d" % t)
            identb = T([128, 128], BF16, 512)
            identres = Res()
            S.op("dve", lambda e: e.tensor_copy(out=identb[:], in_=ident[:]), R=[constres], W=[identres], deps=[fence1])
            wsres, bspres = Res(), Res()
            wsres.w = fence1
            S.op("pool", lambda e: e.dma_start(out=wsT[:].rearrange("p g i -> p (g i)"), in_=wsT_d[:, :]), W=[wsres], key="wsT")
            S.op("dve", lambda e: e.memset(wsT[64:128, :, 0:64], 0.0), W=[wsres])
            S.op("sp", lambda e: e.dma_start(out=bsp[:], in_=bsp_d[:, :]), W=[bspres], key="bsp")

            if not do0:
                l1_ids = l1_plan()
            id_v1, id_u1, id_g1, out_ids1 = l1_ids

            for t in range(8):
                for n in range(4):
                    pres[t][n].w = fence1
            bnres = [Res() for _ in range(8)]

            def pre1_part(t, part):
                prenorm_tile(1, t, x1[:, t, :], x1res[t], False, XT1, xt1res, junk1, [junk1res] + tmpres, hT1, hres1, t % 2, pool_rstd=True,
                             part=part)

            _p1 = {"started": False}

            def pre1(t):
                if not _p1["started"]:
                    pre1_part(0, "front")
                    _p1["started"] = True
                pre1_part(t, "back")
                if t + 1 < 8:
                    pre1_part(t + 1, "front")

            def v_mm(n, t, wb, wr):
                return proj_tok(wb, wr, 512, hT1, hres1[t], t)

            def v_ev(n, t, pb):
                S.op("act", lambda e: e.activation(out=P[:, t, n, :], in_=banks[pb][:], func=AF.Gelu_apprx_tanh),
                     R=[bres[pb]], W=[pres[t][n]])
                S.op("dve", lambda e: e.bn_stats(out=bnst[:, t, n, :], in_=P[:, t, n, :]), R=[pres[t][n]], W=[bnres[t]])

            def v_unit(n, t, wb, wr):
                v_ev(n, t, v_mm(n, t, wb, wr))

            lnres_t = [Res() for _ in range(8)]
            gch = tmpA[1]
            bch = T([128, 512], F32, A + 164 * KB)
            bchres = Res()
            bchres.w = fence1
            utmp = tmpA[0]

            def load_gb(n):
                S.op("sp", lambda e: e.dma_start(out=gch[:], in_=lng_d[:, n * 512:(n + 1) * 512]), W=[tmpres[1]], key="lng")
                S.op("sp", lambda e: e.dma_start(out=bch[:], in_=lnb_d[:, n * 512:(n + 1) * 512]), W=[bchres], key="lnb")

            def ln_unit(n, t):
                S.op("act", lambda e: e.activation(out=P[:, t, n, :], in_=P[:, t, n, :], func=AF.Identity,
                                                   scale=lnsc[:, t, 0:1], bias=lnsc[:, t, 1:2]),
                     R=[lnres_t[t]], W=[pres[t][n]])
                S.op("dve", lambda e: e.tensor_tensor(out=P[:, t, n, :], in0=P[:, t, n, :], in1=gch[:], op=ALU.mult),
                     R=[tmpres[1]], W=[pres[t][n]])
                S.op("dve", lambda e: e.tensor_tensor(out=Pbf[:, t, n, 0:512], in0=P[:, t, n, :], in1=bch[:], op=ALU.add),
                     R=[bchres], W=[pres[t][n]])

            def sp_unit(n, t):
                ub = UPB[n % 2]

                def sfn(e):
                    for gg in range(2):
                        g = n * 2 + gg
                        ins = e.matmul(banks[ub][:, gg * 256:(gg + 1) * 256], lhsT=wsT[:, g, :],
                                       rhs=Pbf[:, t, n, gg * 256:(gg + 1) * 256], start=True, stop=True)
                    return ins
                S.op("pe", sfn, R=[pres[t][n], wsres], W=[bres[ub]])
                for gg in range(2):
                    g = n * 2 + gg
                    if n % 2 == 0:
                        S.op("dve", lambda e, gg=gg, g=g: e.tensor_scalar(
                            out=P[:, t, n, gg * 256:(gg + 1) * 256], in0=banks[ub][:, gg * 256:(gg + 1) * 256],
                            scalar1=bsp[:, g:g + 1], scalar2=None, op0=ALU.add), R=[bres[ub], bspres], W=[pres[t][n]])
                    else:
                        S.op("act", lambda e, gg=gg, g=g: e.activation(
                            out=P[:, t, n, gg * 256:(gg + 1) * 256], in_=banks[ub][:, gg * 256:(gg + 1) * 256],
                            func=AF.Identity, bias=bsp[:, g:g + 1]), R=[bres[ub], bspres], W=[pres[t][n]])

            def u_mm(n, t, wb, wr):
                return proj_tok(wb, wr, 512, hT1, hres1[t], t)

            def u_ev(n, t, pb):
                S.op("act", lambda e: e.activation(out=utmp[:], in_=banks[pb][:], func=AF.Gelu_apprx_tanh),
                     R=[bres[pb]], W=[tmpres[0]])
                S.op("dve", lambda e: e.tensor_tensor(out=P[:, t, n, :], in0=P[:, t, n, :], in1=utmp[:], op=ALU.mult),
                     R=[tmpres[0]], W=[pres[t][n]])

            def ln_stats(t):
                S.op("dve", lambda e: e.bn_aggr(out=mv[:, t, :], in_=bnst[:, t, :, :].rearrange("p a b -> p (a b)")),
                     R=[bnres[t]], W=[lnres_t[t]])
                S.op("pool", lambda e: e.tensor_scalar(out=lnsc[:, t, 0:1], in0=mv[:, t, 1:2], scalar1=EPS, scalar2=None, op0=ALU.add),
                     R=[lnres_t[t]], W=[lnres_t[t]])
                S.op("pool", lambda e: e.tensor_tensor(out=lnsc[:, t, 0:1], in0=lnsc[:, t, 0:1], in1=neghalf[:, 0:1], op=ALU.pow),
                     R=[lnres_t[t], constres], W=[lnres_t[t]])
                S.op("dve", lambda e: e.scalar_tensor_tensor(out=lnsc[:, t, 1:2], in0=mv[:, t, 0:1], scalar=-1.0, in1=lnsc[:, t, 0:1],
                                                             op0=ALU.mult, op1=ALU.mult), R=[lnres_t[t]], W=[lnres_t[t]])

            pre1(0)
            pre1(1)
            for n in range(4):
                wb, wr = WS.get(id_v1[n])
                if n == 3:
                    load_gb(0)
                for t in range(8):
                    if n == 3:
                        pb3 = v_mm(n, t, wb, wr)
                        if t > 0:
                            ln_stats(t - 1)
                            ln_unit(0, t - 1)
                        if t > 1:
                            sp_unit(0, t - 2)
                        v_ev(n, t, pb3)
                        continue
                    v_unit(n, t, wb, wr)
                    if n == 0 and t + 2 < 8:
                        pre1(t + 2)
                    if n == 0 and t == 5:
                        pf = S.op("dve", lambda e: e.memset(stat[:, 0:1], 0.0), deps=[S.q[e_][-1] for e_ in ENGS if S.q[e_]])
                        for tt_ in (6, 7):
                            for nn_ in range(4):
                                pres[tt_][nn_].w = pf
            ln_stats(7)
            ln_unit(0, 7)
            sp_unit(0, 6)
            sp_unit(0, 7)
            cut(24)
            for n in range(4):
                wb, wr = WS.get(id_u1[n])
                if n + 1 < 4:
                    load_gb(n + 1)
                for t in range(8):
                    pbu = u_mm(n, t, wb, wr)
                    if n + 1 < 4:
                        ln_unit(n + 1, t)
                        if t > 0:
                            sp_unit(n + 1, t - 1)
                    u_ev(n, t, pbu)
                if n + 1 < 4:
                    sp_unit(n + 1, 7)
            cut(25)
            def g_tail(t, n):
                tb = next_tr()

                def tfn(e):
                    for j in range(4):
                        ins = e.matmul(banks[tb][:, j * 128:(j + 1) * 128], lhsT=Pbf[:, t, n, j * 128:(j + 1) * 128], rhs=identb[:],
                                       start=True, stop=True)
                    return ins
                S.op("pe", tfn, R=[pres[t][n], identres], W=[bres[tb]])
                S.op("act", lambda e: e.activation(out=Pbf[:, t, n, 512:1024], in_=banks[tb][:], func=AF.Copy),
                     R=[bres[tb]], W=[pres[t][n]])

            pend = None
            for n in range(4):
                wb, wr = WS.get(id_g1[n])
                for t in range(8):
                    pb = proj_tok(wb, wr, 512, hT1, hres1[t], t)
                    if pend is not None:
                        g_tail(*pend)
                    ti = t % 2
                    S.op("act", lambda e, pb=pb, ti=ti: e.activation(out=tmpA[ti][:], in_=banks[pb][:], func=AF.Silu),
                         R=[bres[pb]], W=[tmpres[ti]])
                    S.op("dve", lambda e, t=t, n=n, ti=ti: e.tensor_tensor(out=Pbf[:, t, n, 0:512], in0=P[:, t, n, :], in1=tmpA[ti][:], op=ALU.mult),
                         R=[tmpres[ti]], W=[pres[t][n]])
                    pend = (t, n)
            g_tail(*pend)
            cut(26)
            lastrecs = [S.q[e][-1] for e in ENGS if S.q[e]]
            fence2 = S.op("dve", lambda e: e.memset(stat[:, 0:1], 0.0), deps=lastrecs)
            ybuf1 = [T([128, 2048], F32, A + 0 + i * 8 * KB) for i in range(4)]
            ybres1 = [Res() for _ in range(8)]
            ncres1 = [Res(), Res()]
            junk4 = T([128, 512], BF16, 2304)
            junk4res = Res()
            for r in ybres1 + ncres1 + [junk4res]:
                r.w = fence2
            outs = []

            def yfrag(t):
                return P[:, 2 * (t - 4):2 * (t - 4) + 2, :, 0:256]

            def yslice1(tt, n):
                if tt < 4:
                    return ybuf1[tt][:, n * 512:(n + 1) * 512], (lambda ap: ap)
                yo = P[:, 2 * (tt - 4) + n // 2, (n % 2) * 2:(n % 2) * 2 + 2, 0:256]
                return yo, (lambda ap: ap.rearrange("p (a c) -> p a c", a=2))

            def finish1(t, tt, rstd_ap, rres):
                if t < 4:
                    yt, xv, ov = ybuf1[t][:], x1[:, t, :], out_d[t * 128:(t + 1) * 128, :]
                else:
                    yt = yfrag(t)
                    xv = x1[:, t, :].rearrange("p (a b c) -> p a b c", a=2, b=4)
                    ov = out_d[t * 128:(t + 1) * 128, :].rearrange("p (a b c) -> p a b c", a=2, b=4)
                S.op("dve", lambda e: e.scalar_tensor_tensor(out=yt, in0=yt, scalar=rstd_ap, in1=xv, op0=ALU.mult, op1=ALU.add),
                     R=[rres, x1res[t]], W=[ybres1[tt]])
                outs.append(S.op("sp", lambda e: e.dma_start(out=ov, in_=yt), R=[ybres1[tt]], key="o%d" % t))

            outproj(1, lambda k, t: Pbf[:, t, k // 4, 512 + (k % 4) * 128:512 + (k % 4 + 1) * 128], lambda t: pres[t], out_ids1, ybuf1, ybres1,
                    tmpA, tmpres, junk4, junk4res, finish1, ngrp=1, yslice=yslice1)


    except _Cut:
        pass
    for eng in ("sp",):
        tails = [r for r in S.q["sp"] if r.dma and r.semkey[1].startswith("o")]
        S.op("sp", None, deps=tails)
    S.emit()
    es.close()
    return nc


_CACHE = {}


def _get(mode):
    if mode not in _CACHE:
        _CACHE[mode] = build(mode)
    return _CACHE[mode]


def _consts():
    ident = np.eye(128, dtype=np.float32)
    s = np.arange(128)[:, None]
    t = np.arange(128)[None, :]
    umat = np.where((s > t) & (s // 64 == t // 64), -1.0 / 16.0, 0.0).astype(np.float32)
    ind = np.zeros((128, 2), np.float32)
    ind[:64, 0] = -1.0 / 16.0
    ind[64:, 1] = -1.0 / 16.0
    umatp = np.where(s > t, -1.0 / 16.0, 0.0).astype(np.float32)
    indp = np.zeros((128, 2), np.float32)
    indp[:, 0] = -1.0 / 16.0
    return ident, umat, ind, umatp, indp


def _fm(v):
    return np.ascontiguousarray(np.asarray(v, np.float32).reshape(16, 128).T)


FUSED = True


def kernel(x, norm_pre, norm_post, gla_w_in, gla_w_gate2, gla_b_gate, gla_o_gain, gla_w_out,
           sgu_w_in, sgu_ln_gain, sgu_ln_bias, sgu_w_spatial, sgu_b_spatial, sgu_w_out):
    f = lambda a: np.ascontiguousarray(np.asarray(a, dtype=np.float32))
    x = f(x)
    ident, umat, ind, umatp, indp = _consts()
    gpre = np.concatenate([_fm(norm_pre[0]), _fm(norm_pre[1])], axis=1)
    npost = np.ascontiguousarray(np.broadcast_to(f(norm_post)[:, None, :], (2, 128, 2048)))
    waug = np.concatenate([f(gla_w_gate2)[0], f(gla_b_gate)[0][None, :]], axis=0)
    common = {"ident": ident, "gpre": gpre, "npost": npost}
    l0 = {"win0": f(gla_w_in)[0], "waug": waug, "wout0": f(gla_w_out)[0], "ogain": _fm(gla_o_gain[0]),
          "umat": umat, "ind": ind, "umatp": umatp, "indp": indp}
    wsT = np.ascontiguousarray(np.transpose(f(sgu_w_spatial)[0], (2, 0, 1)).reshape(128, 1024))
    l1 = {"win1": f(sgu_w_in)[0], "wout1": f(sgu_w_out)[0],
          "lng": np.ascontiguousarray(np.broadcast_to(f(sgu_ln_gain)[0][None, :], (128, 2048))),
          "lnb": np.ascontiguousarray(np.broadcast_to(f(sgu_ln_bias)[0][None, :], (128, 2048))),
          "wsT": wsT, "bsp": np.ascontiguousarray(f(sgu_b_spatial)[0].T)}
    zeros = np.zeros((1024, 2048), np.float32)
    cores = list(range(8))

    def shard(xx):
        return [np.ascontiguousarray(xx[c // 2, (c % 2) * 1024:(c % 2 + 1) * 1024]) for c in cores]

    def prevs(xx):
        return [np.ascontiguousarray(xx[c // 2, 0:1024]) if c % 2 == 1 else zeros for c in cores]

    xs, xps = shard(x), prevs(x)
    if FUSED:
        nc = _get("fused")
        maps = [dict(common, **l0, **l1, x=xs[c], xp=xps[c]) for c in cores]
        res = run_bass_kernel_spmd(nc, maps, core_ids=cores)
        outs = [r["out"] for r in res.results]
    else:
        nc0 = _get("l0")
        maps = [dict(common, **l0, x=xs[c], xp=xps[c]) for c in cores]
        res = run_bass_kernel_spmd(nc0, maps, core_ids=cores)
        x1s = [r["out"] for r in res.results]
        nc1 = _get("l1")
        maps = [dict(common, **l1, x=x1s[c]) for c in cores]
        res = run_bass_kernel_spmd(nc1, maps, core_ids=cores)
        outs = [r["out"] for r in res.results]
    out = np.empty((4, 2048, 2048), np.float32)
    for c in cores:
        out[c // 2, (c % 2) * 1024:(c % 2 + 1) * 1024] = outs[c]
    return out
```

```python
from contextlib import ExitStack
import numpy as np
import concourse.bass as bass
import concourse.mybir as mybir
from concourse.bass_utils import run_bass_kernel_spmd

F32 = mybir.dt.float32
BF16 = mybir.dt.bfloat16
U8 = mybir.dt.uint8
AF = mybir.ActivationFunctionType
ALU = mybir.AluOpType

EPS = 1e-6
KB = 1024
CUT = [0]
DBG = {}


class _Cut(Exception):
    pass


def cut(n):
    if CUT[0] == n:
        raise _Cut()


class Rec:
    __slots__ = ("eng", "fn", "deps", "sig", "semkey", "cnt", "dma")

    def __init__(self, eng, fn, deps, semkey, dma):
        self.eng = eng
        self.fn = fn
        self.deps = deps
        self.sig = False
        self.semkey = semkey
        self.cnt = 0
        self.dma = dma


class Res:
    __slots__ = ("w", "r")

    def __init__(self):
        self.w = None
        self.r = {}


ENGS = ("pe", "act", "dve", "pool", "sp")


class Sched:
    def __init__(self, nc):
        self.nc = nc
        self.q = {e: [] for e in ENGS}

    def op(self, eng, fn, R=(), W=(), deps=(), key=None):
        d = [x for x in deps if x is not None]
        for res in R:
            if res.w is not None:
                d.append(res.w)
        for res in W:
            if res.w is not None:
                d.append(res.w)
            d.extend(res.r.values())
        semkey = eng if key is None else ("dma", key)
        rec = Rec(eng, fn, d, semkey, key is not None)
        self.q[eng].append(rec)
        for res in R:
            res.r[semkey] = rec
        for res in W:
            res.w = rec
            res.r = {}
        return rec

    def emit(self):
        nc = self.nc
        for e in ENGS:
            for r in self.q[e]:
                if r.dma:
                    r.sig = True
                for d in r.deps:
                    d.sig = True
        counts = {}
        for e in ENGS:
            for r in self.q[e]:
                if r.sig:
                    c = counts.get(r.semkey, 0) + (16 if r.dma else 1)
                    counts[r.semkey] = c
                    r.cnt = c
        keys = list(counts.keys())
        with ExitStack() as es:
            sems = {}
            for i, k in enumerate(keys):
                sems[k] = es.enter_context(nc.semaphore("s%d" % i))
            block = es.enter_context(nc.Block())

            def run(ename):
                def body(eng):
                    waited = {}
                    for r in self.q[ename]:
                        need = {}
                        for d in r.deps:
                            if d.cnt > need.get(d.semkey, 0):
                                need[d.semkey] = d.cnt
                        for k, c in need.items():
                            if c > waited.get(k, 0):
                                eng.wait_ge(sems[k], c)
                                waited[k] = c
                        if r.fn is None:
                            continue
                        ins = r.fn(eng)
                        if r.sig:
                            ins.then_inc(sems[r.semkey], 16 if r.dma else 1)
                return body

            block.tensor(run("pe"))
            block.scalar(run("act"))
            block.vector(run("dve"))
            block.gpsimd(run("pool"))
            block.sync(run("sp"))
        return len(keys)


def build(mode="fused"):
    nc = bass.Bass("TRN2", target_bir_lowering=False)
    S = Sched(nc)
    do0 = mode in ("l0", "fused")
    do1 = mode in ("l1", "fused")

    def dram(name, shape, kind="ExternalInput"):
        return nc.dram_tensor(name, list(shape), F32, kind=kind).ap()

    x_d = dram("x", [1024, 2048])
    ident_d = dram("ident", [128, 128])
    gpre_d = dram("gpre", [128, 32])
    npost_d = dram("npost", [2, 128, 2048])
    if do0:
        xp_d = dram("xp", [1024, 2048])
        win0_d = dram("win0", [2048, 6160])
        waug_d = dram("waug", [17, 1024])
        wout0_d = dram("wout0", [2048, 2048])
        ogain_d = dram("ogain", [128, 16])
        umat_d = dram("umat", [128, 128])
        ind_d = dram("ind", [128, 2])
        umatp_d = dram("umatp", [128, 128])
        indp_d = dram("indp", [128, 2])
    if do1:
        win1_d = dram("win1", [2048, 6144])
        wout1_d = dram("wout1", [2048, 2048])
        lng_d = dram("lng", [128, 2048])
        lnb_d = dram("lnb", [128, 2048])
        wsT_d = dram("wsT", [128, 1024])
        bsp_d = dram("bsp", [128, 8])
    out_d = dram("out", [1024, 2048], kind="ExternalOutput")

    A_OFF = 40 * KB
    base = (nc.sbuf_base + 31) // 32 * 32
    ARENA = 206 * KB
    nc.alloc_sbuf_tensor("arena", [128, ARENA], U8)
    _cnt = [0]

    def T(shape, dt, off):
        _cnt[0] += 1
        assert off % 32 == 0
        esz = 4 if dt == F32 else 2
        n = 1
        for s in shape[1:]:
            n *= s
        assert off + n * esz <= ARENA, (shape, off)
        return nc.alloc_sbuf_tensor_at("t%d" % _cnt[0], list(shape), dt, offset=base + off)

    ident = T([128, 128], F32, 0)
    umat = T([128, 128], F32, 512)
    ind = T([128, 2], F32, 1024)
    gpre = T([128, 2, 16], F32, 1056)
    ogain = T([128, 16], F32, 1184)
    bsp = T([128, 8], F32, 1248)
    stat = T([128, 256], F32, 1280)
    bnst = T([128, 8, 4, 6], F32, 2304)
    gaug = T([128, 1024], BF16, 3328)
    waug = T([128, 1024], BF16, 5376)
    decay = T([128, 8, 8, 2], F32, 7424)
    neghalf = T([128, 16], F32, 7936)
    epsc = T([128, 8], F32, 8000)
    wsT = T([128, 8, 128], BF16, 3328)
    mv = T([128, 8, 2], F32, 5376)
    lnsc = T([128, 8, 2], F32, 5440)
    wbufs = [T([128, 16, 512], BF16, 8 * KB + i * 16 * KB) for i in range(2)]
    wres = [Res(), Res()]
    A = 40 * KB

    es = ExitStack()
    banks = [es.enter_context(nc.psum_tensor("pb%d" % i, [128, 512], F32)) for i in range(8)]
    bres = [Res() for _ in range(8)]
    PJ = [0, 1, 2]
    TRB = [3, 4]
    UPB = [5, 6]
    OBK = 7
    pj_i = [0]
    tr_i = [0]

    def next_pj():
        b = PJ[pj_i[0] % len(PJ)]
        pj_i[0] += 1
        return b

    def next_tr():
        b = TRB[tr_i[0] % len(TRB)]
        tr_i[0] += 1
        return b

    st_i = [1]

    def stat_slot(n=1):
        i = st_i[0]
        assert i + n <= 256, "stat slots exhausted"
        st_i[0] = i + n
        return i

    cres = Res()

    def ld(dst, src, key):
        S.op("act", lambda e: e.dma_start(out=dst, in_=src), W=[cres], key=key)

    ld(ident[:], ident_d[:, :], "c0")
    ld(gpre[:].rearrange("p a b -> p (a b)"), gpre_d[:, :], "c1")
    if do0:
        ld(umat[:], umat_d[:, :], "c2")
        ld(ind[:], ind_d[:, :], "c3")
        umatP = T([128, 128], F32, A_OFF + 160 * KB)
        indP = T([128, 2], F32, 8032)
        ld(umatP[:], umatp_d[:, :], "c5")
        ld(indP[:], indp_d[:, :], "c6")
        ld(ogain[:], ogain_d[:, :], "c4")
    S.op("dve", lambda e: e.memset(epsc[:], EPS))
    cbar = S.op("dve", lambda e: e.memset(neghalf[:], -0.5), R=[], W=[],
                deps=[r for r in S.q["act"] if r.dma])
    constres = Res()
    constres.w = cbar

    wq = {"i": 0}

    def wload(src2d, ncols, pad=None):
        i = wq["i"] % 2
        wq["i"] += 1
        wb = wbufs[i]
        src = src2d.rearrange("(k p) n -> p k n", p=128)
        S.op("pool", lambda e: e.dma_start(out=wb[:, :, 0:ncols], in_=src), W=[wres[i]], key="w%d" % i)
        if pad is not None:
            psrc, pn = pad
            psrc = psrc.rearrange("(k p) n -> p k n", p=128)
            rec = S.op("pool", lambda e: e.dma_start(out=wb[:, :, ncols:ncols + pn], in_=psrc), key="w%d" % i)
            wres[i].w = rec
        return wb, wres[i]

    class WStream:
        def __init__(self):
            self.plan = []
            self.issued = []

        def add(self, src2d, ncols, pad=None):
            self.plan.append((src2d, ncols, pad))
            return len(self.plan) - 1

        def warm(self):
            while len(self.issued) < min(2, len(self.plan)):
                s_, n_, pad_ = self.plan[len(self.issued)]
                self.issued.append(wload(s_, n_, pad_))

        def get(self, j):
            assert j == getattr(self, "last", -1) + 1, "weight chunks must be consumed in plan order"
            self.last = j
            while len(self.issued) <= min(j + 1, len(self.plan) - 1):
                s, n, pad = self.plan[len(self.issued)]
                self.issued.append(wload(s, n, pad))
            return self.issued[j]

    WS = WStream()

    def rstd_from_ss(ss_ap, n_elems, out_ap, R, W):
        S.op("act", lambda e: e.activation(out=out_ap, in_=ss_ap, func=AF.Ln, scale=1.0 / n_elems, bias=epsc[:, 0:1]), R=R + [constres], W=W)
        S.op("act", lambda e: e.activation(out=out_ap, in_=out_ap, func=AF.Exp, scale=-0.5), R=W, W=W)

    def prenorm_tile(layer, t, src_ap, src_res, from_dram, XT, xtres, junk, junkres, hT, hres, i, dma_extra_w=(), pool_rstd=False,
                     part="both"):
        xt = XT[i]
        if part == "back":
            return prenorm_back(layer, t, xt, xtres, hT, hres, i)
        if from_dram:
            S.op("sp", lambda e: e.dma_start(out=xt[:], in_=src_ap), W=[xtres[i]] + list(dma_extra_w), key="xt%d" % i)
            xin, xin_res = xt[:], xtres[i]
        else:
            xin, xin_res = src_ap, src_res
        sl = stat_slot(2)
        sres = Res()
        S.op("act", lambda e: e.activation(out=junk[:], in_=xin, func=AF.Square, accum_out=stat[:, sl:sl + 1]),
             R=[xin_res, constres], W=(list(junkres) if isinstance(junkres, (list, tuple)) else [junkres]) + [sres])
        if pool_rstd:
            S.op("pool", lambda e: e.tensor_scalar(out=stat[:, sl + 1:sl + 2], in0=stat[:, sl:sl + 1], scalar1=1.0 / 2048.0, scalar2=EPS,
                                                   op0=ALU.mult, op1=ALU.add), R=[sres], W=[sres])
            S.op("pool", lambda e: e.tensor_tensor(out=stat[:, sl + 1:sl + 2], in0=stat[:, sl + 1:sl + 2], in1=neghalf[:, 0:1], op=ALU.pow),
                 R=[sres, constres], W=[sres])
        else:
            rstd_from_ss(stat[:, sl:sl + 1], 2048.0, stat[:, sl + 1:sl + 2], [sres], [sres])
        S.op("dve", lambda e: e.tensor_scalar(out=xt[:], in0=xin, scalar1=stat[:, sl + 1:sl + 2], scalar2=None,
                                              op0=ALU.mult), R=[xin_res, sres], W=[xtres[i]])
        if part == "front":
            return
        prenorm_back(layer, t, xt, xtres, hT, hres, i)

    def prenorm_back(layer, t, xt, xtres, hT, hres, i):
        for b in range(4):
            tb = next_tr()

            def tfn(e, b=b, tb=tb):
                for j in range(4):
                    ins = e.transpose(out=banks[tb][:, j * 128:(j + 1) * 128],
                                      in_=xt[:, (4 * b + j) * 128:(4 * b + j + 1) * 128], identity=ident[:])
                return ins
            S.op("pe", tfn, R=[xtres[i], constres], W=[bres[tb]])
            S.op("dve", lambda e, b=b, tb=tb: e.tensor_tensor(
                out=hT[:, 4 * b:4 * b + 4, t * 128:(t + 1) * 128],
                in0=banks[tb][:].rearrange("p (j c) -> p j c", j=4),
                in1=gpre[:, layer, 4 * b:4 * b + 4].unsqueeze(2).to_broadcast([128, 4, 128]),
                op=ALU.mult), R=[bres[tb], constres], W=[hres[t][b]])

    def proj_tok(wb, wr, ncols, hT, hres_t, t):
        pb = next_pj()

        def fn(e):
            for k in range(16):
                ins = e.matmul(banks[pb][:, 0:ncols], lhsT=hT[:, k, t * 128:(t + 1) * 128], rhs=wb[:, k, 0:ncols],
                               start=(k == 0), stop=(k == 15))
            return ins
        S.op("pe", fn, R=[wr] + hres_t, W=[bres[pb]])
        return pb

    def outproj_plan(wout_d, ngrp=2):
        ids = {}
        for grp in range(ngrp):
            for n in range(4):
                ids[(grp, n)] = WS.add(wout_d[:, n * 512:(n + 1) * 512], 512)
        return ids

    def l1_plan():
        CU, CVV, CGG = 0, 2048, 4096
        id_v1 = [WS.add(win1_d[:, CVV + n * 512:CVV + (n + 1) * 512], 512) for n in range(4)]
        id_u1 = [WS.add(win1_d[:, CU + n * 512:CU + (n + 1) * 512], 512) for n in range(4)]
        id_g1 = [WS.add(win1_d[:, CGG + n * 512:CGG + (n + 1) * 512], 512) for n in range(4)]
        return id_v1, id_u1, id_g1, outproj_plan(wout1_d, 1)

    def outproj(layer, oT_lhsT, oT_res, ids, ybuf, ybres, nchunk, ncres, junk, junkres, finish, ngrp=2, yslice=None):
        TG = 8 // ngrp
        for grp in range(ngrp):
            yss = stat_slot(4 * TG)
            yssres = [Res() for _ in range(TG)]

            def fin(tt, grp=grp, yss=yss, yssres=yssres):
                t = grp * TG + tt
                sl = stat_slot(2)
                r2 = Res()
                S.op("dve", lambda e: e.reduce_sum(out=stat[:, sl:sl + 1], in_=stat[:, yss + tt * 4:yss + tt * 4 + 4],
                                                   axis=mybir.AxisListType.X), R=[yssres[tt]], W=[r2])
                rstd_from_ss(stat[:, sl:sl + 1], 2048.0, stat[:, sl + 1:sl + 2], [r2], [r2])
                finish(t, tt, stat[:, sl + 1:sl + 2], r2)

            for n in range(4):
                wb, wr = WS.get(ids[(grp, n)])
                nci = n % 2
                S.op("sp", lambda e, n=n, nci=nci: e.dma_start(out=nchunk[nci][:], in_=npost_d[layer, :, n * 512:(n + 1) * 512]),
                     W=[ncres[nci]], key="nc%d" % nci)
                for tt in range(TG):
                    t = grp * TG + tt
                    pb = next_pj()

                    def fn(e, t=t, pb=pb, wb=wb):
                        for k in range(16):
                            ins = e.matmul(banks[pb][:], lhsT=oT_lhsT(k, t), rhs=wb[:, k, :], start=(k == 0), stop=(k == 15))
                        return ins
                    S.op("pe", fn, R=[wr] + oT_res(t), W=[bres[pb]])
                    sq = S.op("act", lambda e, pb=pb, tt=tt, n=n, yss=yss: e.activation(out=junk[:, 0:512], in_=banks[pb][:], func=AF.Square,
                                                                            accum_out=stat[:, yss + tt * 4 + n:yss + tt * 4 + n + 1]),
                         R=[bres[pb]], W=[junkres, yssres[tt]])
                    if yslice is None:
                        yo, vw = ybuf[tt][:, n * 512:(n + 1) * 512], (lambda ap: ap)
                    else:
                        yo, vw = yslice(tt, n)
                    S.op("dve", lambda e, pb=pb, nci=nci, yo=yo, vw=vw: e.tensor_tensor(
                        out=yo, in0=vw(banks[pb][:]), in1=vw(nchunk[nci][:]), op=ALU.mult),
                        R=[bres[pb], ncres[nci]], W=[ybres[tt]], deps=[sq])
                    if n == 3 and tt > 0:
                        fin(tt - 1)
            fin(TG - 1)

    x1 = T([128, 8, 2048], F32, A + 100 * KB)
    x1res = [Res() for _ in range(8)]

    try:
        if do0:
            hT = T([128, 16, 1024], BF16, A + 0)
            kexp = T([128, 8, 1024], F32, A + 32 * KB)
            oT = T([128, 16, 1024], BF16, A + 32 * KB)
            kdec = T([128, 8, 1024], BF16, A + 64 * KB)
            vEO = [T([128, 8, 512], BF16, A + 80 * KB + i * 8 * KB) for i in range(2)]
            Sst = T([128, 4, 2, 512], F32, A + 96 * KB)
            Sbf = [T([128, 2, 512], BF16, A + 112 * KB + i * 2 * KB) for i in range(2)]
            qEO = [T([128, 2, 8, 128], BF16, A + 116 * KB + i * 4 * KB) for i in range(2)]
            sg = [T([128, 8, 512], F32, A + 124 * KB + i * 16 * KB) for i in range(2)]
            onb = [T([128, 512], F32, A + 156 * KB + i * 2 * KB) for i in range(2)]
            wglr = T([128, 16, 128], BF16, A + 156 * KB)
            XT = [T([128, 2048], F32, A + 124 * KB + i * 8 * KB) for i in range(2)]
            junk = T([128, 2048], BF16, A + 140 * KB)
            lbuf = T([128, 512], F32, A + 144 * KB)
            lbufs = [lbuf, T([128, 512], F32, A + 148 * KB)]
            lress = [Res(), Res()]
            ebuf = T([128, 512], F32, A + 146 * KB)
            etmp = [T([128, 512], F32, A + 160 * KB + i * 2 * KB) for i in range(2)]

            hres = [[Res() for _ in range(4)] for _ in range(8)]
            xtres = [Res(), Res()]
            junkres = Res()
            kexpres = [[Res(), Res()] for _ in range(8)]
            kdres = [[Res(), Res()] for _ in range(8)]
            vres = [Res() for _ in range(8)]
            sres = [[Res(), Res()] for _ in range(4)]
            sbres = [[Res(), Res()] for _ in range(2)]
            qres = [Res() for _ in range(4)]
            sgres = [[Res() for _ in range(8)] for _ in range(2)]
            onres = [Res(), Res()]
            oTres = [[Res() for _ in range(4)] for _ in range(8)]
            decres = [[Res(), Res()] for _ in range(8)]
            gaugres = [Res(), Res()]
            lres, eres = Res(), Res()
            etres = [Res(), Res()]
            aliasfence = Res()

            CQ, CK, CV, CG, CL = 0, 1024, 2048, 4096, 6144

            def gla_plan(prev):
                id_glr = None
                id_k, id_v, id_q, id_g = [], [], [], []
                for h in range(4):
                    if h == 1:
                        id_k = [WS.add(win0_d[:, CK + n * 512:CK + (n + 1) * 512], 512) for n in range(2)]
                    if not prev and h > 0:
                        id_g.append(WS.add(win0_d[:, CG + h * 512:CG + (h + 1) * 512], 512))
                    id_v.append(WS.add(win0_d[:, CV + h * 512:CV + (h + 1) * 512], 512))
                    if not prev:
                        id_q.append(WS.add(win0_d[:, CQ + h * 256:CQ + (h + 1) * 256], 256))
                    if not prev and h == 0:
                        id_g.append(WS.add(win0_d[:, CG + h * 512:CG + (h + 1) * 512], 512))
                return id_glr, id_k, id_v, id_q, id_g

            gla_plans = {True: gla_plan(True), False: gla_plan(False)}
            out_ids0 = outproj_plan(wout0_d, 1)
            if do1:
                l1_ids = l1_plan()

            WS.warm()
            waugres = Res()
            S.op("dve", lambda e: e.memset(waug[:], 0.0), W=[waugres])
            S.op("pool", lambda e: e.dma_start(out=waug[0:17, :], in_=waug_d[:, :]), W=[waugres], key="waug")
            S.op("pool", lambda e: e.memset(Sst[:].rearrange("p a b c -> p (a b c)"), 0.0), W=[r for hs in sres for r in hs])


            def gla_phase(prev, carry=None):
                src_d = xp_d if prev else x_d
                id_glr, id_k, id_v, id_q, id_g = gla_plans[prev]
                kd_all = [r for t_ in range(8) for r in kdres[t_]]

                def prenorm_gen():
                    def pt(t, part):
                        prenorm_tile(0, t, src_d[t * 128:(t + 1) * 128, :], None, True, XT, xtres, junk, junkres, hT, hres, t % 2,
                                     dma_extra_w=(), part=part)
                    pt(0, "front")
                    for t in range(8):
                        pt(t, "back")
                        if t + 1 < 8:
                            pt(t + 1, "front")
                        yield
                S.op("dve", lambda e: e.memset(gaug[:, :], 0.0), W=gaugres)
                S.op("dve", lambda e: e.memset(gaug[0:32, :], 1.0), W=gaugres)
                wb_glr = wglr
                wr_glr = Res()
                glr_src = win0_d[:, CL:CL + 16].rearrange("(k p) n -> p k n", p=128)
                pad_src = win0_d[:, 0:112].rearrange("(k p) n -> p k n", p=128)
                S.op("pool", lambda e: e.dma_start(out=wglr[:, :, 0:16], in_=glr_src), W=[wr_glr, onres[0], onres[1]], key="wglr")
                S.op("pool", lambda e: e.dma_start(out=wglr[:, :, 16:128], in_=pad_src), W=[wr_glr, onres[0], onres[1]], key="wglr")

                def glr_group(G):
                    pb = next_pj()

                    def fn(e):
                        for k in range(16):
                            ins = e.matmul(banks[pb][:, :], lhsT=wb_glr[:, k, 0:128], rhs=hT[:, k, G * 512:(G + 1) * 512],
                                           start=(k == 0), stop=(k == 15))
                        return ins
                    S.op("pe", fn, R=[wr_glr, onres[0], onres[1]] + [r for t in range(4 * G, 4 * G + 4) for r in hres[t]], W=[bres[pb]])
                    S.op("act", lambda e: e.activation(out=gaug[0:16, G * 512:(G + 1) * 512], in_=banks[pb][0:16, :], func=AF.Copy),
                         R=[bres[pb]], W=[gaugres[G]])
                def gating_gen():
                    GA, GB, GC = UPB[0], UPB[1], OBK

                    def stage_b(t, half, li):
                        S.op("pe", lambda e: e.matmul(banks[GB][:], lhsT=(umatP if prev else umat)[:], rhs=lbufs[li][:], start=True, stop=True),
                             R=[lress[li], constres], W=[bres[GB]])

                        def totfn(e):
                            for j in range(4):
                                ins = e.matmul(banks[GC][:, 2 * j:2 * j + 2], lhsT=lbufs[li][:, j * 128:(j + 1) * 128], rhs=(indP if prev else ind)[:],
                                               start=True, stop=True)
                            return ins
                        S.op("pe", totfn, R=[lress[li], constres], W=[bres[GC]])
                        S.op("act", lambda e: e.activation(out=kexp[:, t, half * 512:(half + 1) * 512], in_=banks[GB][:], func=AF.Exp),
                             R=[bres[GB]], W=[kexpres[t][half]])
                        S.op("act", lambda e: e.activation(
                            out=decay[:, t, half * 4:half * 4 + 4, :], in_=banks[GC][:, 0:8].rearrange("p (j c) -> p j c", j=4), func=AF.Exp),
                            R=[bres[GC]], W=[decres[t][half]])

                    pend = None
                    i = 0
                    for t in range(8):
                        for half in range(2):
                            li = i % 2
                            i += 1
                            S.op("pe", lambda e, t=t, half=half: e.matmul(banks[GA][:], lhsT=gaug[:, t * 128:(t + 1) * 128],
                                                                           rhs=waug[:, half * 512:(half + 1) * 512], start=True, stop=True),
                                 R=[gaugres[t // 4], waugres], W=[bres[GA]])
                            S.op("act", lambda e: e.activation(out=ebuf[:], in_=banks[GA][:], func=AF.Exp, scale=-1.0),
                                 R=[bres[GA], aliasfence], W=[eres])
                            S.op("act", lambda e, li=li: e.activation(out=lbufs[li][:], in_=ebuf[:], func=AF.Ln, bias=1.0),
                                 R=[eres, aliasfence], W=[lress[li]])
                            if pend is not None:
                                stage_b(*pend)
                            pend = (t, half, li)
                            yield
                    stage_b(*pend)
                    yield

                cut(4)
                def stream_inproj(h):
                    hb = h % 2

                    def part_g():
                        wb, wr = WS.get(id_g[h])
                        for t in range(8):
                            pb = proj_tok(wb, wr, 512, hT, hres[t], t)
                            ei = t % 2
                            if h > 0:
                                S.op("act", lambda e, pb=pb, t=t: e.activation(out=sg[hb][:, t, :], in_=banks[pb][:], func=AF.Silu),
                                     R=[bres[pb]], W=[sgres[hb][t]] + ([aliasfence] if hb == 1 else []))
                                yield
                                continue
                            S.op("act", lambda e, pb=pb, ei=ei: e.activation(out=etmp[ei][:], in_=banks[pb][:], func=AF.Exp, scale=-1.0),
                                 R=[bres[pb]], W=[etres[ei]])
                            S.op("act", lambda e, ei=ei: e.activation(out=etmp[ei][:], in_=etmp[ei][:], func=AF.Ln, bias=1.0),
                                 R=[etres[ei]], W=[etres[ei]])
                            S.op("act", lambda e, ei=ei: e.activation(out=etmp[ei][:], in_=etmp[ei][:], func=AF.Exp, scale=-1.0),
                                 R=[etres[ei]], W=[etres[ei]])
                            S.op("dve", lambda e, pb=pb, t=t, ei=ei: e.tensor_tensor(out=sg[hb][:, t, :], in0=banks[pb][:], in1=etmp[ei][:], op=ALU.mult),
                                 R=[bres[pb], etres[ei]], W=[sgres[hb][t]] + ([aliasfence] if hb == 1 else []))
                            yield

                    def part_v():
                        wb, wr = WS.get(id_v[h])
                        for t in range(8):
                            pb = proj_tok(wb, wr, 512, hT, hres[t], t)
                            if prev:
                                S.op("act", lambda e, pb=pb, t=t: e.activation(out=vEO[0][:, t, :], in_=banks[pb][:], func=AF.Copy),
                                     R=[bres[pb]], W=[vres[t]])
                                yield
                                continue
                            S.op("act", lambda e, pb=pb, t=t: e.activation(out=vEO[0][0:64, t, :], in_=banks[pb][0:64, :], func=AF.Copy),
                                 R=[bres[pb]], W=[vres[t]])
                            S.op("act", lambda e, pb=pb, t=t: e.activation(out=vEO[1][64:128, t, :], in_=banks[pb][64:128, :], func=AF.Copy),
                                 R=[bres[pb]], W=[vres[t]])
                            yield

                    def part_q():
                        wb, wr = WS.get(id_q[h])
                        for s in range(2):
                            for G in range(2):
                                pb = next_pj()

                                def fn(e, s=s, G=G, pb=pb, wb=wb):
                                    for k in range(16):
                                        ins = e.matmul(banks[pb][:], lhsT=wb[:, k, s * 128:(s + 1) * 128], rhs=hT[:, k, G * 512:(G + 1) * 512],
                                                       start=(k == 0), stop=(k == 15))
                                    return ins
                                S.op("pe", fn, R=[wr] + [r for t in range(4 * G, 4 * G + 4) for r in hres[t]], W=[bres[pb]])
                                pv = banks[pb][:].rearrange("p (t c) -> p t c", t=4)
                                S.op("dve", lambda e, s=s, G=G, pv=pv: e.tensor_scalar(out=qEO[0][:, s, 4 * G:4 * G + 4, 0:64], in0=pv[:, :, 0:64],
                                                                                        scalar1=1.0 / 16.0, scalar2=None, op0=ALU.mult),
                                     R=[bres[pb]], W=[qres[s * 2 + G]])
                                S.op("dve", lambda e, s=s, G=G, pv=pv: e.tensor_scalar(out=qEO[1][:, s, 4 * G:4 * G + 4, 64:128], in0=pv[:, :, 64:128],
                                                                                        scalar1=1.0 / 16.0, scalar2=None, op0=ALU.mult),
                                     R=[bres[pb]], W=[qres[s * 2 + G]])
                                yield


                    if prev:
                        yield from part_v()
                    elif h == 0:
                        yield from part_v()
                        yield from part_q()
                        yield from part_g()
                    else:
                        yield from part_g()
                        yield from part_v()
                        yield from part_q()

                def stream_scan(h):
                    hb = h % 2

                    def o_mm(c):
                        t, hh = c // 2, c % 2
                        ci = c % 2

                        def ofn(e):
                            for s_ in range(2):
                                ins = e.matmul(banks[OBK][:], lhsT=qEO[hh][:, s_, t, :], rhs=Sbf[ci][:, s_, :],
                                               start=(hh == 0 and s_ == 0), stop=(hh == 1 and s_ == 1))
                            return ins
                        S.op("pe", ofn, R=[qres[0 + (c // 8)], qres[2 + (c // 8)], sbres[ci][0], sbres[ci][1]], W=[obres])

                    def opost_stats(t):
                        sl = stat_slot(2)
                        r2 = Res()
                        S.op("act", lambda e: e.activation(out=junk2[:], in_=banks[OBK][:], func=AF.Square, accum_out=stat[:, sl:sl + 1]),
                             R=[obres], W=[junk2res, r2])
                        S.op("pool", lambda e: e.tensor_scalar(out=stat[:, sl + 1:sl + 2], in0=stat[:, sl:sl + 1], scalar1=1.0 / 512.0, scalar2=EPS,
                                                               op0=ALU.mult, op1=ALU.add), R=[r2], W=[r2])
                        S.op("pool", lambda e: e.tensor_tensor(out=stat[:, sl + 1:sl + 2], in0=stat[:, sl + 1:sl + 2], in1=neghalf[:, 0:1], op=ALU.pow),
                             R=[r2, constres], W=[r2])
                        return sl, r2

                    def opost_apply(t, sl, r2):
                        oi = t % 2
                        S.op("dve", lambda e: e.scalar_tensor_tensor(
                            out=onb[oi][:], in0=banks[OBK][:], scalar=stat[:, sl + 1:sl + 2], in1=sg[hb][:, t, :],
                            op0=ALU.mult, op1=ALU.mult), R=[obres, r2, sgres[hb][t]], W=[onres[oi]])

                    def opost_tr(t):
                        oi = t % 2
                        tb = next_tr()

                        def tfn(e):
                            for j in range(4):
                                ins = e.transpose(out=banks[tb][:, j * 128:(j + 1) * 128], in_=onb[oi][:, j * 128:(j + 1) * 128],
                                                  identity=ident[:])
                            return ins
                        S.op("pe", tfn, R=[onres[oi], constres], W=[bres[tb]])
                        return tb

                    def opost_evac(t, tb):
                        S.op("dve", lambda e: e.tensor_tensor(
                            out=oT[:, 4 * h:4 * h + 4, t * 128:(t + 1) * 128],
                            in0=banks[tb][:].rearrange("p (j c) -> p j c", j=4),
                            in1=ogain[:, 4 * h:4 * h + 4].unsqueeze(2).to_broadcast([128, 4, 128]),
                            op=ALU.mult), R=[bres[tb], constres], W=[oTres[t][h]] + kexpres_all)

                    pend_o = None
                    pend_tr = None
                    for c in range(17 if not prev else 8):
                        t, hh = (c // 2, c % 2) if not prev else (c, 0)
                        do_upd = c < 16
                        if do_upd:
                            for s in range(2):
                                ub = (TRB[s] if (prev and c % 2 == 1) else UPB[s])
                                S.op("pe", lambda e, t=t, hh=hh, s=s, ub=ub: e.matmul(
                                    banks[ub][:], lhsT=kdec[:, t, h * 256 + s * 128:h * 256 + (s + 1) * 128],
                                    rhs=vEO[hh][:, t, :], start=True, stop=True),
                                    R=[kdres[t][h // 2], vres[t]], W=[bres[ub]])
                        o_now = pend_o
                        if o_now is not None:
                            o_mm(o_now)
                        tb = opost_tr(pend_tr) if pend_tr is not None else None
                        if do_upd:
                            for s in range(2):
                                ub = (TRB[s] if (prev and c % 2 == 1) else UPB[s])
                                S.op("dve", lambda e, t=t, hh=hh, s=s, ub=ub: e.scalar_tensor_tensor(
                                    out=Sst[:, h, s, :], in0=Sst[:, h, s, :], scalar=decay[:, t, h * 2 + s, hh:hh + 1], in1=banks[ub][:],
                                    op0=ALU.mult, op1=ALU.add), R=[bres[ub], decres[t][h // 2]], W=[sres[h][s]])
                        if tb is not None:
                            opost_evac(pend_tr, tb)
                            pend_tr = None
                        stats = None
                        if o_now is not None and o_now % 2 == 1:
                            stats = opost_stats(o_now // 2)
                        if do_upd and not prev:
                            ci = c % 2
                            for s in range(2):
                                S.op("act", lambda e, s=s, ci=ci: e.activation(out=Sbf[ci][:, s, :], in_=Sst[:, h, s, :], func=AF.Copy),
                                     R=[sres[h][s]], W=[sbres[ci][s]])
                        if stats is not None:
                            opost_apply(o_now // 2, *stats)
                            pend_tr = o_now // 2
                        pend_o = c if (do_upd and not prev) else None
                        yield
                    if pend_tr is not None:
                        tb = opost_tr(pend_tr)
                        opost_evac(pend_tr, tb)
                        yield

                def drain(g):
                    for _ in g:
                        pass

                def merge(a, b, na=1, nb=1):
                    done_a = done_b = False
                    while not (done_a and done_b):
                        for _ in range(nb):
                            if not done_b:
                                try:
                                    next(b)
                                except StopIteration:
                                    done_b = True
                        for _ in range(na):
                            if not done_a:
                                try:
                                    next(a)
                                except StopIteration:
                                    done_a = True

                if carry is not None:
                    S.op("pool", lambda e: e.memset(vEO[1][0:64, :, :], 0.0), W=vres)
                    S.op("pool", lambda e: e.memset(qEO[0][:, :, :, 64:128], 0.0), W=qres)
                    S.op("pool", lambda e: e.memset(qEO[1][:, :, :, 0:64], 0.0), W=qres)
                pre = prenorm_gen()
                for _ in range(2):
                    next(pre)
                    if carry is not None:
                        for _c in range(4):
                            next(carry, None)
                if carry is not None:
                    drain(carry)
                    S.op("pool", lambda e: e.memset(vEO[0][64:128, :, :], 0.0), W=vres)
                ga, gb = stream_inproj(0), gating_gen()
                for _ in range(2):
                    next(ga)
                    next(pre)
                glr_group(0)
                for _ in range(4):
                    next(gb)
                    next(gb)
                    next(ga)
                    next(pre)
                glr_group(1)
                merge(ga, gb, 1, 2 if prev else 1)
                for n in range(2):
                    wb, wr = WS.get(id_k[n])
                    for t in range(8):
                        pb = proj_tok(wb, wr, 512, hT, hres[t], t)
                        S.op("dve", lambda e, pb=pb, t=t, n=n: e.tensor_tensor(out=kdec[:, t, n * 512:(n + 1) * 512], in0=banks[pb][:],
                                                                                in1=kexp[:, t, n * 512:(n + 1) * 512], op=ALU.mult),
                             R=[bres[pb], kexpres[t][n]], W=[kdres[t][n]])


                cut(41)
                if CUT[0] == 42:
                    drain(stream_scan(0))
                    cut(42)
                for h in range(4):
                    if h + 1 < 4:
                        merge(stream_inproj(h + 1), stream_scan(h), 1, 1)
                    elif prev:
                        return stream_scan(h)
                    else:
                        drain(stream_scan(h))

            obres = bres[OBK]
            DBG.update(oT=oT.name, kdec=kdec.name, vE=vEO[0].name, vO=vEO[1].name, qE=qEO[0].name, qO=qEO[1].name, Sst=Sst.name, decay=decay.name, gaug=gaug.name, hT=hT.name)
            junk2 = T([128, 512], BF16, 2304)
            junk2res = Res()
            kexpres_all = [r for t in range(8) for r in kexpres[t]]

            cut(1)
            carry0 = gla_phase(True)
            cut(5)
            gla_phase(False, carry0)
            cut(6)

            ybuf = [x1[:, t, :] for t in range(8)]
            ybres = x1res
            nchunk = [T([128, 512], F32, A + 0 + i * 2 * KB) for i in range(2)]
            ncres = [Res(), Res()]
            XR = [T([128, 2048], F32, A + 8 * KB + i * 8 * KB) for i in range(2)]
            xrres = [Res(), Res()]
            junk3 = T([128, 512], BF16, A + 24 * KB)
            junk3res = Res()
            lastrecs = [S.q[e][-1] for e in ENGS if S.q[e]]
            fence0 = S.op("dve", lambda e: e.memset(stat[:, 0:1], 0.0), deps=lastrecs)
            for r in ybres + ncres + xrres + [junk3res] + x1res:
                r.w = fence0

            def finish0(t, tt, rstd_ap, rres):
                i = t % 2
                S.op("sp", lambda e: e.dma_start(out=XR[i][:], in_=x_d[t * 128:(t + 1) * 128, :]), W=[xrres[i]], key="xr%d" % i)
                S.op("dve", lambda e: e.scalar_tensor_tensor(out=x1[:, t, :], in0=x1[:, t, :], scalar=rstd_ap, in1=XR[i][:],
                                                             op0=ALU.mult, op1=ALU.add),
                     R=[ybres[tt], rres, xrres[i]], W=[x1res[t]])
                if mode == "l0":
                    S.op("sp", lambda e: e.dma_start(out=out_d[t * 128:(t + 1) * 128, :], in_=x1[:, t, :]), R=[x1res[t]], key="o%d" % t)

            outproj(0, lambda k, t: oT[:, k, t * 128:(t + 1) * 128], lambda t: oTres[t], out_ids0, ybuf, ybres, nchunk, ncres,
                    junk3, junk3res, finish0, ngrp=1)

        if do1:
            lastrecs = [S.q[e][-1] for e in ENGS if S.q[e]]
            fence1 = S.op("dve", lambda e: e.memset(stat[:, 0:1], 0.0), deps=lastrecs)
            hT1 = T([128, 16, 1024], BF16, A + 0)
            P = T([128, 8, 4, 512], F32, A + 32 * KB)
            Pbf = P.bitcast(BF16) if hasattr(P, "bitcast") else None
            tmpA = [T([128, 512], F32, A + 96 * KB + i * 2 * KB) for i in range(2)]
            XT1 = [T([128, 2048], F32, A + 32 * KB + (6 + i) * 8 * KB) for i in range(2)]
            junk1 = T([128, 2048], BF16, A + 96 * KB)
            hres1 = [[Res() for _ in range(4)] for _ in range(8)]
            xt1res = [Res(), Res()]
            junk1res = Res()
            pres = [[Res() for _ in range(4)] for _ in range(8)]
            tmpres = [Res(), Res()]
            for r in [x for hs in hres1 for x in hs] + xt1res + [junk1res] + tmpres:
                r.w = fence1
            if mode == "l1":
                for t in range(8):
                    S.op("sp", lambda e, t=t: e.dma_start(out=x1[:, t, :], in_=x_d[t * 128:(t + 1) * 128, :]), W=[x1res[t]], key="x1l%d" % t)
            identb = T([128, 128], BF16, 512)
            identres = Res()
            S.op("dve", lambda e: e.tensor_copy(out=identb[:], in_=ident[:]), R=[constres], W=[identres], deps=[fence1])
            wsres, bspres = Res(), Res()
            wsres.w = fence1
            S.op("pool", lambda e: e.dma_start(out=wsT[:].rearrange("p g i -> p (g i)"), in_=wsT_d[:, :]), W=[wsres], key="wsT")
            S.op("dve", lambda e: e.memset(wsT[64:128, :, 0:64], 0.0), W=[wsres])
            S.op("sp", lambda e: e.dma_start(out=bsp[:], in_=bsp_d[:, :]), W=[bspres], key="bsp")

            if not do0:
                l1_ids = l1_plan()
            id_v1, id_u1, id_g1, out_ids1 = l1_ids

            for t in range(8):
                for n in range(4):
                    pres[t][n].w = fence1
            bnres = [Res() for _ in range(8)]

            def pre1_part(t, part):
                prenorm_tile(1, t, x1[:, t, :], x1res[t], False, XT1, xt1res, junk1, [junk1res] + tmpres, hT1, hres1, t % 2, pool_rstd=True,
                             part=part)

            _p1 = {"started": False}

            def pre1(t):
                if not _p1["started"]:
                    pre1_part(0, "front")
                    _p1["started"] = True
                pre1_part(t, "back")
                if t + 1 < 8:
                    pre1_part(t + 1, "front")

            def v_mm(n, t, wb, wr):
                return proj_tok(wb, wr, 512, hT1, hres1[t], t)

            def v_ev(n, t, pb):
                S.op("act", lambda e: e.activation(out=P[:, t, n, :], in_=banks[pb][:], func=AF.Gelu_apprx_tanh),
                     R=[bres[pb]], W=[pres[t][n]])
                S.op("dve", lambda e: e.bn_stats(out=bnst[:, t, n, :], in_=P[:, t, n, :]), R=[pres[t][n]], W=[bnres[t]])

            def v_unit(n, t, wb, wr):
                v_ev(n, t, v_mm(n, t, wb, wr))

            lnres_t = [Res() for _ in range(8)]
            gch = tmpA[1]
            bch = T([128, 512], F32, A + 164 * KB)
            bchres = Res()
            bchres.w = fence1
            utmp = tmpA[0]

            def load_gb(n):
                S.op("sp", lambda e: e.dma_start(out=gch[:], in_=lng_d[:, n * 512:(n + 1) * 512]), W=[tmpres[1]], key="lng")
                S.op("sp", lambda e: e.dma_start(out=bch[:], in_=lnb_d[:, n * 512:(n + 1) * 512]), W=[bchres], key="lnb")

            def ln_unit(n, t):
                S.op("act", lambda e: e.activation(out=P[:, t, n, :], in_=P[:, t, n, :], func=AF.Identity,
                                                   scale=lnsc[:, t, 0:1], bias=lnsc[:, t, 1:2]),
                     R=[lnres_t[t]], W=[pres[t][n]])
                S.op("dve", lambda e: e.tensor_tensor(out=P[:, t, n, :], in0=P[:, t, n, :], in1=gch[:], op=ALU.mult),
                     R=[tmpres[1]], W=[pres[t][n]])
                S.op("dve", lambda e: e.tensor_tensor(out=Pbf[:, t, n, 0:512], in0=P[:, t, n, :], in1=bch[:], op=ALU.add),
                     R=[bchres], W=[pres[t][n]])

            def sp_unit(n, t):
                ub = UPB[n % 2]

                def sfn(e):
                    for gg in range(2):
                        g = n * 2 + gg
                        ins = e.matmul(banks[ub][:, gg * 256:(gg + 1) * 256], lhsT=wsT[:, g, :],
                                       rhs=Pbf[:, t, n, gg * 256:(gg + 1) * 256], start=True, stop=True)
                    return ins
                S.op("pe", sfn, R=[pres[t][n], wsres], W=[bres[ub]])
                for gg in range(2):
                    g = n * 2 + gg
                    if n % 2 == 0:
                        S.op("dve", lambda e, gg=gg, g=g: e.tensor_scalar(
                            out=P[:, t, n, gg * 256:(gg + 1) * 256], in0=banks[ub][:, gg * 256:(gg + 1) * 256],
                            scalar1=bsp[:, g:g + 1], scalar2=None, op0=ALU.add), R=[bres[ub], bspres], W=[pres[t][n]])
                    else:
                        S.op("act", lambda e, gg=gg, g=g: e.activation(
                            out=P[:, t, n, gg * 256:(gg + 1) * 256], in_=banks[ub][:, gg * 256:(gg + 1) * 256],
                            func=AF.Identity, bias=bsp[:, g:g + 1]), R=[bres[ub], bspres], W=[pres[t][n]])

            def u_mm(n, t, wb, wr):
                return proj_tok(wb, wr, 512, hT1, hres1[t], t)

            def u_ev(n, t, pb):
                S.op("act", lambda e: e.activation(out=utmp[:], in_=banks[pb][:], func=AF.Gelu_apprx_tanh),
                     R=[bres[pb]], W=[tmpres[0]])
                S.op("dve", lambda e: e.tensor_tensor(out=P[:, t, n, :], in0=P[:, t, n, :], in1=utmp[:], op=ALU.mult),
                     R=[tmpres[0]], W=[pres[t][n]])

            def ln_stats(t):
                S.op("dve", lambda e: e.bn_aggr(out=mv[:, t, :], in_=bnst[:, t, :, :].rearrange("p a b -> p (a b)")),
                     R=[bnres[t]], W=[lnres_t[t]])
                S.op("pool", lambda e: e.tensor_scalar(out=lnsc[:, t, 0:1], in0=mv[:, t, 1:2], scalar1=EPS, scalar2=None, op0=ALU.add),
                     R=[lnres_t[t]], W=[lnres_t[t]])
                S.op("pool", lambda e: e.tensor_tensor(out=lnsc[:, t, 0:1], in0=lnsc[:, t, 0:1], in1=neghalf[:, 0:1], op=ALU.pow),
                     R=[lnres_t[t], constres], W=[lnres_t[t]])
                S.op("dve", lambda e: e.scalar_tensor_tensor(out=lnsc[:, t, 1:2], in0=mv[:, t, 0:1], scalar=-1.0, in1=lnsc[:, t, 0:1],
                                                             op0=ALU.mult, op1=ALU.mult), R=[lnres_t[t]], W=[lnres_t[t]])

            pre1(0)
            pre1(1)
            for n in range(4):
                wb, wr = WS.get(id_v1[n])
                if n == 3:
                    load_gb(0)
                for t in range(8):
                    if n == 3:
                        pb3 = v_mm(n, t, wb, wr)
                        if t > 0:
                            ln_stats(t - 1)
                            ln_unit(0, t - 1)
                        if t > 1:
                            sp_unit(0, t - 2)
                        v_ev(n, t, pb3)
                        continue
                    v_unit(n, t, wb, wr)
                    if n == 0 and t + 2 < 8:
                        pre1(t + 2)
                    if n == 0 and t == 5:
                        pf = S.op("dve", lambda e: e.memset(stat[:, 0:1], 0.0), deps=[S.q[e_][-1] for e_ in ENGS if S.q[e_]])
                        for tt_ in (6, 7):
                            for nn_ in range(4):
                                pres[tt_][nn_].w = pf
            ln_stats(7)
            ln_unit(0, 7)
            sp_unit(0, 6)
            sp_unit(0, 7)
            cut(24)
            for n in range(4):
                wb, wr = WS.get(id_u1[n])
                if n + 1 < 4:
                    load_gb(n + 1)
                for t in range(8):
                    pbu = u_mm(n, t, wb, wr)
                    if n + 1 < 4:
                        ln_unit(n + 1, t)
                        if t > 0:
                            sp_unit(n + 1, t - 1)
                    u_ev(n, t, pbu)
                if n + 1 < 4:
                    sp_unit(n + 1, 7)
            cut(25)
            def g_tail(t, n):
                tb = next_tr()

                def tfn(e):
                    for j in range(4):
                        ins = e.matmul(banks[tb][:, j * 128:(j + 1) * 128], lhsT=Pbf[:, t, n, j * 128:(j + 1) * 128], rhs=identb[:],
                                       start=True, stop=True)
                    return ins
                S.op("pe", tfn, R=[pres[t][n], identres], W=[bres[tb]])
                S.op("act", lambda e: e.activation(out=Pbf[:, t, n, 512:1024], in_=banks[tb][:], func=AF.Copy),
                     R=[bres[tb]], W=[pres[t][n]])

            pend = None
            for n in range(4):
                wb, wr = WS.get(id_g1[n])
                for t in range(8):
                    pb = proj_tok(wb, wr, 512, hT1, hres1[t], t)
                    if pend is not None:
                        g_tail(*pend)
                    ti = t % 2
                    S.op("act", lambda e, pb=pb, ti=ti: e.activation(out=tmpA[ti][:], in_=banks[pb][:], func=AF.Silu),
                         R=[bres[pb]], W=[tmpres[ti]])
                    S.op("dve", lambda e, t=t, n=n, ti=ti: e.tensor_tensor(out=Pbf[:, t, n, 0:512], in0=P[:, t, n, :], in1=tmpA[ti][:], op=ALU.mult),
                         R=[tmpres[ti]], W=[pres[t][n]])
                    pend = (t, n)
            g_tail(*pend)
            cut(26)
            lastrecs = [S.q[e][-1] for e in ENGS if S.q[e]]
            fence2 = S.op("dve", lambda e: e.memset(stat[:, 0:1], 0.0), deps=lastrecs)
            ybuf1 = [T([128, 2048], F32, A + 0 + i * 8 * KB) for i in range(4)]
            ybres1 = [Res() for _ in range(8)]
            ncres1 = [Res(), Res()]
            junk4 = T([128, 512], BF16, 2304)
            junk4res = Res()
            for r in ybres1 + ncres1 + [junk4res]:
                r.w = fence2
            outs = []

            def yfrag(t):
                return P[:, 2 * (t - 4):2 * (t - 4) + 2, :, 0:256]

            def yslice1(tt, n):
                if tt < 4:
                    return ybuf1[tt][:, n * 512:(n + 1) * 512], (lambda ap: ap)
                yo = P[:, 2 * (tt - 4) + n // 2, (n % 2) * 2:(n % 2) * 2 + 2, 0:256]
                return yo, (lambda ap: ap.rearrange("p (a c) -> p a c", a=2))

            def finish1(t, tt, rstd_ap, rres):
                if t < 4:
                    yt, xv, ov = ybuf1[t][:], x1[:, t, :], out_d[t * 128:(t + 1) * 128, :]
                else:
                    yt = yfrag(t)
                    xv = x1[:, t, :].rearrange("p (a b c) -> p a b c", a=2, b=4)
                    ov = out_d[t * 128:(t + 1) * 128, :].rearrange("p (a b c) -> p a b c", a=2, b=4)
                S.op("dve", lambda e: e.scalar_tensor_tensor(out=yt, in0=yt, scalar=rstd_ap, in1=xv, op0=ALU.mult, op1=ALU.add),
                     R=[rres, x1res[t]], W=[ybres1[tt]])
                outs.append(S.op("sp", lambda e: e.dma_start(out=ov, in_=yt), R=[ybres1[tt]], key="o%d" % t))

            outproj(1, lambda k, t: Pbf[:, t, k // 4, 512 + (k % 4) * 128:512 + (k % 4 + 1) * 128], lambda t: pres[t], out_ids1, ybuf1, ybres1,
                    tmpA, tmpres, junk4, junk4res, finish1, ngrp=1, yslice=yslice1)


    except _Cut:
        pass
    for eng in ("sp",):
        tails = [r for r in S.q["sp"] if r.dma and r.semkey[1].startswith("o")]
        S.op("sp", None, deps=tails)
    S.emit()
    es.close()
    return nc


_CACHE = {}


def _get(mode):
    if mode not in _CACHE:
        _CACHE[mode] = build(mode)
    return _CACHE[mode]


def _consts():
    ident = np.eye(128, dtype=np.float32)
    s = np.arange(128)[:, None]
    t = np.arange(128)[None, :]
    umat = np.where((s > t) & (s // 64 == t // 64), -1.0 / 16.0, 0.0).astype(np.float32)
    ind = np.zeros((128, 2), np.float32)
    ind[:64, 0] = -1.0 / 16.0
    ind[64:, 1] = -1.0 / 16.0
    umatp = np.where(s > t, -1.0 / 16.0, 0.0).astype(np.float32)
    indp = np.zeros((128, 2), np.float32)
    indp[:, 0] = -1.0 / 16.0
    return ident, umat, ind, umatp, indp


def _fm(v):
    return np.ascontiguousarray(np.asarray(v, np.float32).reshape(16, 128).T)


FUSED = True


def kernel(x, norm_pre, norm_post, gla_w_in, gla_w_gate2, gla_b_gate, gla_o_gain, gla_w_out,
           sgu_w_in, sgu_ln_gain, sgu_ln_bias, sgu_w_spatial, sgu_b_spatial, sgu_w_out):
    f = lambda a: np.ascontiguousarray(np.asarray(a, dtype=np.float32))
    x = f(x)
    ident, umat, ind, umatp, indp = _consts()
    gpre = np.concatenate([_fm(norm_pre[0]), _fm(norm_pre[1])], axis=1)
    npost = np.ascontiguousarray(np.broadcast_to(f(norm_post)[:, None, :], (2, 128, 2048)))
    waug = np.concatenate([f(gla_w_gate2)[0], f(gla_b_gate)[0][None, :]], axis=0)
    common = {"ident": ident, "gpre": gpre, "npost": npost}
    l0 = {"win0": f(gla_w_in)[0], "waug": waug, "wout0": f(gla_w_out)[0], "ogain": _fm(gla_o_gain[0]),
          "umat": umat, "ind": ind, "umatp": umatp, "indp": indp}
    wsT = np.ascontiguousarray(np.transpose(f(sgu_w_spatial)[0], (2, 0, 1)).reshape(128, 1024))
    l1 = {"win1": f(sgu_w_in)[0], "wout1": f(sgu_w_out)[0],
          "lng": np.ascontiguousarray(np.broadcast_to(f(sgu_ln_gain)[0][None, :], (128, 2048))),
          "lnb": np.ascontiguousarray(np.broadcast_to(f(sgu_ln_bias)[0][None, :], (128, 2048))),
          "wsT": wsT, "bsp": np.ascontiguousarray(f(sgu_b_spatial)[0].T)}
    zeros = np.zeros((1024, 2048), np.float32)
    cores = list(range(8))

    def shard(xx):
        return [np.ascontiguousarray(xx[c // 2, (c % 2) * 1024:(c % 2 + 1) * 1024]) for c in cores]

    def prevs(xx):
        return [np.ascontiguousarray(xx[c // 2, 0:1024]) if c % 2 == 1 else zeros for c in cores]

    xs, xps = shard(x), prevs(x)
    if FUSED:
        nc = _get("fused")
        maps = [dict(common, **l0, **l1, x=xs[c], xp=xps[c]) for c in cores]
        res = run_bass_kernel_spmd(nc, maps, core_ids=cores)
        outs = [r["out"] for r in res.results]
    else:
        nc0 = _get("l0")
        maps = [dict(common, **l0, x=xs[c], xp=xps[c]) for c in cores]
        res = run_bass_kernel_spmd(nc0, maps, core_ids=cores)
        x1s = [r["out"] for r in res.results]
        nc1 = _get("l1")
        maps = [dict(common, **l1, x=x1s[c]) for c in cores]
        res = run_bass_kernel_spmd(nc1, maps, core_ids=cores)
        outs = [r["out"] for r in res.results]
    out = np.empty((4, 2048, 2048), np.float32)
    for c in cores:
        out[c // 2, (c % 2) * 1024:(c % 2 + 1) * 1024] = outs[c]
    return out
```

```python
from contextlib import ExitStack
import numpy as np
import concourse.bass as bass
import concourse.mybir as mybir
from concourse.bass_utils import run_bass_kernel_spmd

F32 = mybir.dt.float32
BF16 = mybir.dt.bfloat16
U8 = mybir.dt.uint8
AF = mybir.ActivationFunctionType
ALU = mybir.AluOpType

EPS = 1e-6
KB = 1024
CUT = [0]
DBG = {}


class _Cut(Exception):
    pass


def cut(n):
    if CUT[0] == n:
        raise _Cut()


class Rec:
    __slots__ = ("eng", "fn", "deps", "sig", "semkey", "cnt", "dma")

    def __init__(self, eng, fn, deps, semkey, dma):
        self.eng = eng
        self.fn = fn
        self.deps = deps
        self.sig = False
        self.semkey = semkey
        self.cnt = 0
        self.dma = dma


class Res:
    __slots__ = ("w", "r")

    def __init__(self):
        self.w = None
        self.r = {}


ENGS = ("pe", "act", "dve", "pool", "sp")


class Sched:
    def __init__(self, nc):
        self.nc = nc
        self.q = {e: [] for e in ENGS}

    def op(self, eng, fn, R=(), W=(), deps=(), key=None):
        d = [x for x in deps if x is not None]
        for res in R:
            if res.w is not None:
                d.append(res.w)
        for res in W:
            if res.w is not None:
                d.append(res.w)
            d.extend(res.r.values())
        semkey = eng if key is None else ("dma", key)
        rec = Rec(eng, fn, d, semkey, key is not None)
        self.q[eng].append(rec)
        for res in R:
            res.r[semkey] = rec
        for res in W:
            res.w = rec
            res.r = {}
        return rec

    def emit(self):
        nc = self.nc
        for e in ENGS:
            for r in self.q[e]:
                if r.dma:
                    r.sig = True
                for d in r.deps:
                    d.sig = True
        counts = {}
        for e in ENGS:
            for r in self.q[e]:
                if r.sig:
                    c = counts.get(r.semkey, 0) + (16 if r.dma else 1)
                    counts[r.semkey] = c
                    r.cnt = c
        keys = list(counts.keys())
        with ExitStack() as es:
            sems = {}
            for i, k in enumerate(keys):
                sems[k] = es.enter_context(nc.semaphore("s%d" % i))
            block = es.enter_context(nc.Block())

            def run(ename):
                def body(eng):
                    waited = {}
                    for r in self.q[ename]:
                        need = {}
                        for d in r.deps:
                            if d.cnt > need.get(d.semkey, 0):
                                need[d.semkey] = d.cnt
                        for k, c in need.items():
                            if c > waited.get(k, 0):
                                eng.wait_ge(sems[k], c)
                                waited[k] = c
                        if r.fn is None:
                            continue
                        ins = r.fn(eng)
                        if r.sig:
                            ins.then_inc(sems[r.semkey], 16 if r.dma else 1)
                return body

            block.tensor(run("pe"))
            block.scalar(run("act"))
            block.vector(run("dve"))
            block.gpsimd(run("pool"))
            block.sync(run("sp"))
        return len(keys)


def build(mode="fused"):
    nc = bass.Bass("TRN2", target_bir_lowering=False)
    S = Sched(nc)
    do0 = mode in ("l0", "fused")
    do1 = mode in ("l1", "fused")

    def dram(name, shape, kind="ExternalInput"):
        return nc.dram_tensor(name, list(shape), F32, kind=kind).ap()

    x_d = dram("x", [1024, 2048])
    ident_d = dram("ident", [128, 128])
    gpre_d = dram("gpre", [128, 32])
    npost_d = dram("npost", [2, 128, 2048])
    if do0:
        xp_d = dram("xp", [1024, 2048])
        win0_d = dram("win0", [2048, 6160])
        waug_d = dram("waug", [17, 1024])
        wout0_d = dram("wout0", [2048, 2048])
        ogain_d = dram("ogain", [128, 16])
        umat_d = dram("umat", [128, 128])
        ind_d = dram("ind", [128, 2])
        umatp_d = dram("umatp", [128, 128])
        indp_d = dram("indp", [128, 2])
    if do1:
        win1_d = dram("win1", [2048, 6144])
        wout1_d = dram("wout1", [2048, 2048])
        lng_d = dram("lng", [128, 2048])
        lnb_d = dram("lnb", [128, 2048])
        wsT_d = dram("wsT", [128, 1024])
        bsp_d = dram("bsp", [128, 8])
    out_d = dram("out", [1024, 2048], kind="ExternalOutput")

    A_OFF = 40 * KB
    base = (nc.sbuf_base + 31) // 32 * 32
    ARENA = 206 * KB
    nc.alloc_sbuf_tensor("arena", [128, ARENA], U8)
    _cnt = [0]

    def T(shape, dt, off):
        _cnt[0] += 1
        assert off % 32 == 0
        esz = 4 if dt == F32 else 2
        n = 1
        for s in shape[1:]:
            n *= s
        assert off + n * esz <= ARENA, (shape, off)
        return nc.alloc_sbuf_tensor_at("t%d" % _cnt[0], list(shape), dt, offset=base + off)

    ident = T([128, 128], F32, 0)
    umat = T([128, 128], F32, 512)
    ind = T([128, 2], F32, 1024)
    gpre = T([128, 2, 16], F32, 1056)
    ogain = T([128, 16], F32, 1184)
    bsp = T([128, 8], F32, 1248)
    stat = T([128, 256], F32, 1280)
    bnst = T([128, 8, 4, 6], F32, 2304)
    gaug = T([128, 1024], BF16, 3328)
    waug = T([128, 1024], BF16, 5376)
    decay = T([128, 8, 8, 2], F32, 7424)
    neghalf = T([128, 16], F32, 7936)
    epsc = T([128, 8], F32, 8000)
    wsT = T([128, 8, 128], BF16, 3328)
    mv = T([128, 8, 2], F32, 5376)
    lnsc = T([128, 8, 2], F32, 5440)
    wbufs = [T([128, 16, 512], BF16, 8 * KB + i * 16 * KB) for i in range(2)]
    wres = [Res(), Res()]
    A = 40 * KB

    es = ExitStack()
    banks = [es.enter_context(nc.psum_tensor("pb%d" % i, [128, 512], F32)) for i in range(8)]
    bres = [Res() for _ in range(8)]
    PJ = [0, 1, 2]
    TRB = [3, 4]
    UPB = [5, 6]
    OBK = 7
    pj_i = [0]
    tr_i = [0]

    def next_pj():
        b = PJ[pj_i[0] % len(PJ)]
        pj_i[0] += 1
        return b

    def next_tr():
        b = TRB[tr_i[0] % len(TRB)]
        tr_i[0] += 1
        return b

    st_i = [1]

    def stat_slot(n=1):
        i = st_i[0]
        assert i + n <= 256, "stat slots exhausted"
        st_i[0] = i + n
        return i

    cres = Res()

    def ld(dst, src, key):
        S.op("act", lambda e: e.dma_start(out=dst, in_=src), W=[cres], key=key)

    ld(ident[:], ident_d[:, :], "c0")
    ld(gpre[:].rearrange("p a b -> p (a b)"), gpre_d[:, :], "c1")
    if do0:
        ld(umat[:], umat_d[:, :], "c2")
        ld(ind[:], ind_d[:, :], "c3")
        umatP = T([128, 128], F32, A_OFF + 160 * KB)
        indP = T([128, 2], F32, 8032)
        ld(umatP[:], umatp_d[:, :], "c5")
        ld(indP[:], indp_d[:, :], "c6")
        ld(ogain[:], ogain_d[:, :], "c4")
    S.op("dve", lambda e: e.memset(epsc[:], EPS))
    cbar = S.op("dve", lambda e: e.memset(neghalf[:], -0.5), R=[], W=[],
                deps=[r for r in S.q["act"] if r.dma])
    constres = Res()
    constres.w = cbar

    wq = {"i": 0}

    def wload(src2d, ncols, pad=None):
        i = wq["i"] % 2
        wq["i"] += 1
        wb = wbufs[i]
        src = src2d.rearrange("(k p) n -> p k n", p=128)
        S.op("pool", lambda e: e.dma_start(out=wb[:, :, 0:ncols], in_=src), W=[wres[i]], key="w%d" % i)
        if pad is not None:
            psrc, pn = pad
            psrc = psrc.rearrange("(k p) n -> p k n", p=128)
            rec = S.op("pool", lambda e: e.dma_start(out=wb[:, :, ncols:ncols + pn], in_=psrc), key="w%d" % i)
            wres[i].w = rec
        return wb, wres[i]

    class WStream:
        def __init__(self):
            self.plan = []
            self.issued = []

        def add(self, src2d, ncols, pad=None):
            self.plan.append((src2d, ncols, pad))
            return len(self.plan) - 1

        def warm(self):
            while len(self.issued) < min(2, len(self.plan)):
                s_, n_, pad_ = self.plan[len(self.issued)]
                self.issued.append(wload(s_, n_, pad_))

        def get(self, j):
            assert j == getattr(self, "last", -1) + 1, "weight chunks must be consumed in plan order"
            self.last = j
            while len(self.issued) <= min(j + 1, len(self.plan) - 1):
                s, n, pad = self.plan[len(self.issued)]
                self.issued.append(wload(s, n, pad))
            return self.issued[j]

    WS = WStream()

    def rstd_from_ss(ss_ap, n_elems, out_ap, R, W):
        S.op("act", lambda e: e.activation(out=out_ap, in_=ss_ap, func=AF.Ln, scale=1.0 / n_elems, bias=epsc[:, 0:1]), R=R + [constres], W=W)
        S.op("act", lambda e: e.activation(out=out_ap, in_=out_ap, func=AF.Exp, scale=-0.5), R=W, W=W)

    def prenorm_tile(layer, t, src_ap, src_res, from_dram, XT, xtres, junk, junkres, hT, hres, i, dma_extra_w=(), pool_rstd=False,
                     part="both"):
        xt = XT[i]
        if part == "back":
            return prenorm_back(layer, t, xt, xtres, hT, hres, i)
        if from_dram:
            S.op("sp", lambda e: e.dma_start(out=xt[:], in_=src_ap), W=[xtres[i]] + list(dma_extra_w), key="xt%d" % i)
            xin, xin_res = xt[:], xtres[i]
        else:
            xin, xin_res = src_ap, src_res
        sl = stat_slot(2)
        sres = Res()
        S.op("act", lambda e: e.activation(out=junk[:], in_=xin, func=AF.Square, accum_out=stat[:, sl:sl + 1]),
             R=[xin_res, constres], W=(list(junkres) if isinstance(junkres, (list, tuple)) else [junkres]) + [sres])
        if pool_rstd:
            S.op("pool", lambda e: e.tensor_scalar(out=stat[:, sl + 1:sl + 2], in0=stat[:, sl:sl + 1], scalar1=1.0 / 2048.0, scalar2=EPS,
                                                   op0=ALU.mult, op1=ALU.add), R=[sres], W=[sres])
            S.op("pool", lambda e: e.tensor_tensor(out=stat[:, sl + 1:sl + 2], in0=stat[:, sl + 1:sl + 2], in1=neghalf[:, 0:1], op=ALU.pow),
                 R=[sres, constres], W=[sres])
        else:
            rstd_from_ss(stat[:, sl:sl + 1], 2048.0, stat[:, sl + 1:sl + 2], [sres], [sres])
        S.op("dve", lambda e: e.tensor_scalar(out=xt[:], in0=xin, scalar1=stat[:, sl + 1:sl + 2], scalar2=None,
                                              op0=ALU.mult), R=[xin_res, sres], W=[xtres[i]])
        if part == "front":
            return
        prenorm_back(layer, t, xt, xtres, hT, hres, i)

    def prenorm_back(layer, t, xt, xtres, hT, hres, i):
        for b in range(4):
            tb = next_tr()

            def tfn(e, b=b, tb=tb):
                for j in range(4):
                    ins = e.transpose(out=banks[tb][:, j * 128:(j + 1) * 128],
                                      in_=xt[:, (4 * b + j) * 128:(4 * b + j + 1) * 128], identity=ident[:])
                return ins
            S.op("pe", tfn, R=[xtres[i], constres], W=[bres[tb]])
            S.op("dve", lambda e, b=b, tb=tb: e.tensor_tensor(
                out=hT[:, 4 * b:4 * b + 4, t * 128:(t + 1) * 128],
                in0=banks[tb][:].rearrange("p (j c) -> p j c", j=4),
                in1=gpre[:, layer, 4 * b:4 * b + 4].unsqueeze(2).to_broadcast([128, 4, 128]),
                op=ALU.mult), R=[bres[tb], constres], W=[hres[t][b]])

    def proj_tok(wb, wr, ncols, hT, hres_t, t):
        pb = next_pj()

        def fn(e):
            for k in range(16):
                ins = e.matmul(banks[pb][:, 0:ncols], lhsT=hT[:, k, t * 128:(t + 1) * 128], rhs=wb[:, k, 0:ncols],
                               start=(k == 0), stop=(k == 15))
            return ins
        S.op("pe", fn, R=[wr] + hres_t, W=[bres[pb]])
        return pb

    def outproj_plan(wout_d, ngrp=2):
        ids = {}
        for grp in range(ngrp):
            for n in range(4):
                ids[(grp, n)] = WS.add(wout_d[:, n * 512:(n + 1) * 512], 512)
        return ids

    def l1_plan():
        CU, CVV, CGG = 0, 2048, 4096
        id_v1 = [WS.add(win1_d[:, CVV + n * 512:CVV + (n + 1) * 512], 512) for n in range(4)]
        id_u1 = [WS.add(win1_d[:, CU + n * 512:CU + (n + 1) * 512], 512) for n in range(4)]
        id_g1 = [WS.add(win1_d[:, CGG + n * 512:CGG + (n + 1) * 512], 512) for n in range(4)]
        return id_v1, id_u1, id_g1, outproj_plan(wout1_d, 1)

    def outproj(layer, oT_lhsT, oT_res, ids, ybuf, ybres, nchunk, ncres, junk, junkres, finish, ngrp=2, yslice=None):
        TG = 8 // ngrp
        for grp in range(ngrp):
            yss = stat_slot(4 * TG)
            yssres = [Res() for _ in range(TG)]

            def fin(tt, grp=grp, yss=yss, yssres=yssres):
                t = grp * TG + tt
                sl = stat_slot(2)
                r2 = Res()
                S.op("dve", lambda e: e.reduce_sum(out=stat[:, sl:sl + 1], in_=stat[:, yss + tt * 4:yss + tt * 4 + 4],
                                                   axis=mybir.AxisListType.X), R=[yssres[tt]], W=[r2])
                rstd_from_ss(stat[:, sl:sl + 1], 2048.0, stat[:, sl + 1:sl + 2], [r2], [r2])
                finish(t, tt, stat[:, sl + 1:sl + 2], r2)

            for n in range(4):
                wb, wr = WS.get(ids[(grp, n)])
                nci = n % 2
                S.op("sp", lambda e, n=n, nci=nci: e.dma_start(out=nchunk[nci][:], in_=npost_d[layer, :, n * 512:(n + 1) * 512]),
                     W=[ncres[nci]], key="nc%d" % nci)
                for tt in range(TG):
                    t = grp * TG + tt
                    pb = next_pj()

                    def fn(e, t=t, pb=pb, wb=wb):
                        for k in range(16):
                            ins = e.matmul(banks[pb][:], lhsT=oT_lhsT(k, t), rhs=wb[:, k, :], start=(k == 0), stop=(k == 15))
                        return ins
                    S.op("pe", fn, R=[wr] + oT_res(t), W=[bres[pb]])
                    sq = S.op("act", lambda e, pb=pb, tt=tt, n=n, yss=yss: e.activation(out=junk[:, 0:512], in_=banks[pb][:], func=AF.Square,
                                                                            accum_out=stat[:, yss + tt * 4 + n:yss + tt * 4 + n + 1]),
                         R=[bres[pb]], W=[junkres, yssres[tt]])
                    if yslice is None:
                        yo, vw = ybuf[tt][:, n * 512:(n + 1) * 512], (lambda ap: ap)
                    else:
                        yo, vw = yslice(tt, n)
                    S.op("dve", lambda e, pb=pb, nci=nci, yo=yo, vw=vw: e.tensor_tensor(
                        out=yo, in0=vw(banks[pb][:]), in1=vw(nchunk[nci][:]), op=ALU.mult),
                        R=[bres[pb], ncres[nci]], W=[ybres[tt]], deps=[sq])
                    if n == 3 and tt > 0:
                        fin(tt - 1)
            fin(TG - 1)

    x1 = T([128, 8, 2048], F32, A + 100 * KB)
    x1res = [Res() for _ in range(8)]

    try:
        if do0:
            hT = T([128, 16, 1024], BF16, A + 0)
            kexp = T([128, 8, 1024], F32, A + 32 * KB)
            oT = T([128, 16, 1024], BF16, A + 32 * KB)
            kdec = T([128, 8, 1024], BF16, A + 64 * KB)
            vEO = [T([128, 8, 512], BF16, A + 80 * KB + i * 8 * KB) for i in range(2)]
            Sst = T([128, 4, 2, 512], F32, A + 96 * KB)
            Sbf = [T([128, 2, 512], BF16, A + 112 * KB + i * 2 * KB) for i in range(2)]
            qEO = [T([128, 2, 8, 128], BF16, A + 116 * KB + i * 4 * KB) for i in range(2)]
            sg = [T([128, 8, 512], F32, A + 124 * KB + i * 16 * KB) for i in range(2)]
            onb = [T([128, 512], F32, A + 156 * KB + i * 2 * KB) for i in range(2)]
            wglr = T([128, 16, 128], BF16, A + 156 * KB)
            XT = [T([128, 2048], F32, A + 124 * KB + i * 8 * KB) for i in range(2)]
            junk = T([128, 2048], BF16, A + 140 * KB)
            lbuf = T([128, 512], F32, A + 144 * KB)
            lbufs = [lbuf, T([128, 512], F32, A + 148 * KB)]
            lress = [Res(), Res()]
            ebuf = T([128, 512], F32, A + 146 * KB)
            etmp = [T([128, 512], F32, A + 160 * KB + i * 2 * KB) for i in range(2)]

            hres = [[Res() for _ in range(4)] for _ in range(8)]
            xtres = [Res(), Res()]
            junkres = Res()
            kexpres = [[Res(), Res()] for _ in range(8)]
            kdres = [[Res(), Res()] for _ in range(8)]
            vres = [Res() for _ in range(8)]
            sres = [[Res(), Res()] for _ in range(4)]
            sbres = [[Res(), Res()] for _ in range(2)]
            qres = [Res() for _ in range(4)]
            sgres = [[Res() for _ in range(8)] for _ in range(2)]
            onres = [Res(), Res()]
            oTres = [[Res() for _ in range(4)] for _ in range(8)]
            decres = [[Res(), Res()] for _ in range(8)]
            gaugres = [Res(), Res()]
            lres, eres = Res(), Res()
            etres = [Res(), Res()]
            aliasfence = Res()

            CQ, CK, CV, CG, CL = 0, 1024, 2048, 4096, 6144

            def gla_plan(prev):
                id_glr = None
                id_k, id_v, id_q, id_g = [], [], [], []
                for h in range(4):
                    if h == 1:
                        id_k = [WS.add(win0_d[:, CK + n * 512:CK + (n + 1) * 512], 512) for n in range(2)]
                    if not prev and h > 0:
                        id_g.append(WS.add(win0_d[:, CG + h * 512:CG + (h + 1) * 512], 512))
                    id_v.append(WS.add(win0_d[:, CV + h * 512:CV + (h + 1) * 512], 512))
                    if not prev:
                        id_q.append(WS.add(win0_d[:, CQ + h * 256:CQ + (h + 1) * 256], 256))
                    if not prev and h == 0:
                        id_g.append(WS.add(win0_d[:, CG + h * 512:CG + (h + 1) * 512], 512))
                return id_glr, id_k, id_v, id_q, id_g

            gla_plans = {True: gla_plan(True), False: gla_plan(False)}
            out_ids0 = outproj_plan(wout0_d, 1)
            if do1:
                l1_ids = l1_plan()

            WS.warm()
            waugres = Res()
            S.op("dve", lambda e: e.memset(waug[:], 0.0), W=[waugres])
            S.op("pool", lambda e: e.dma_start(out=waug[0:17, :], in_=waug_d[:, :]), W=[waugres], key="waug")
            S.op("pool", lambda e: e.memset(Sst[:].rearrange("p a b c -> p (a b c)"), 0.0), W=[r for hs in sres for r in hs])


            def gla_phase(prev, carry=None, hT_ready=False):
                src_d = xp_d if prev else x_d
                id_glr, id_k, id_v, id_q, id_g = gla_plans[prev]
                kd_all = [r for t_ in range(8) for r in kdres[t_]]

                def prenorm_gen(srcd=src_d):
                    def pt(t, part):
                        prenorm_tile(0, t, srcd[t * 128:(t + 1) * 128, :], None, True, XT, xtres, junk, junkres, hT, hres, t % 2,
                                     dma_extra_w=(), part=part)
                    pt(0, "front")
                    for t in range(8):
                        pt(t, "back")
                        if t + 1 < 8:
                            pt(t + 1, "front")
                        yield
                S.op("dve", lambda e: e.memset(gaug[:, :], 0.0), W=gaugres)
                S.op("dve", lambda e: e.memset(gaug[0:32, :], 1.0), W=gaugres)
                wb_glr = wglr
                wr_glr = Res()
                glr_src = win0_d[:, CL:CL + 16].rearrange("(k p) n -> p k n", p=128)
                pad_src = win0_d[:, 0:112].rearrange("(k p) n -> p k n", p=128)
                S.op("pool", lambda e: e.dma_start(out=wglr[:, :, 0:16], in_=glr_src), W=[wr_glr, onres[0], onres[1]], key="wglr")
                S.op("pool", lambda e: e.dma_start(out=wglr[:, :, 16:128], in_=pad_src), W=[wr_glr, onres[0], onres[1]], key="wglr")

                def glr_group(G):
                    pb = next_pj()

                    def fn(e):
                        for k in range(16):
                            ins = e.matmul(banks[pb][:, :], lhsT=wb_glr[:, k, 0:128], rhs=hT[:, k, G * 512:(G + 1) * 512],
                                           start=(k == 0), stop=(k == 15))
                        return ins
                    S.op("pe", fn, R=[wr_glr, onres[0], onres[1]] + [r for t in range(4 * G, 4 * G + 4) for r in hres[t]], W=[bres[pb]])
                    S.op("act", lambda e: e.activation(out=gaug[0:16, G * 512:(G + 1) * 512], in_=banks[pb][0:16, :], func=AF.Copy),
                         R=[bres[pb]], W=[gaugres[G]])
                def gating_gen():
                    GA, GB, GC = UPB[0], UPB[1], OBK

                    def stage_b(t, half, li):
                        S.op("pe", lambda e: e.matmul(banks[GB][:], lhsT=(umatP if prev else umat)[:], rhs=lbufs[li][:], start=True, stop=True),
                             R=[lress[li], constres], W=[bres[GB]])

                        def totfn(e):
                            for j in range(4):
                                ins = e.matmul(banks[GC][:, 2 * j:2 * j + 2], lhsT=lbufs[li][:, j * 128:(j + 1) * 128], rhs=(indP if prev else ind)[:],
                                               start=True, stop=True)
                            return ins
                        S.op("pe", totfn, R=[lress[li], constres], W=[bres[GC]])
                        S.op("act", lambda e: e.activation(out=kexp[:, t, half * 512:(half + 1) * 512], in_=banks[GB][:], func=AF.Exp),
                             R=[bres[GB]], W=[kexpres[t][half]])
                        S.op("act", lambda e: e.activation(
                            out=decay[:, t, half * 4:half * 4 + 4, :], in_=banks[GC][:, 0:8].rearrange("p (j c) -> p j c", j=4), func=AF.Exp),
                            R=[bres[GC]], W=[decres[t][half]])

                    pend = None
                    i = 0
                    for t in range(8):
                        for half in range(2):
                            li = i % 2
                            i += 1
                            S.op("pe", lambda e, t=t, half=half: e.matmul(banks[GA][:], lhsT=gaug[:, t * 128:(t + 1) * 128],
                                                                           rhs=waug[:, half * 512:(half + 1) * 512], start=True, stop=True),
                                 R=[gaugres[t // 4], waugres], W=[bres[GA]])
                            S.op("act", lambda e: e.activation(out=ebuf[:], in_=banks[GA][:], func=AF.Exp, scale=-1.0),
                                 R=[bres[GA], aliasfence], W=[eres])
                            S.op("act", lambda e, li=li: e.activation(out=lbufs[li][:], in_=ebuf[:], func=AF.Ln, bias=1.0),
                                 R=[eres, aliasfence], W=[lress[li]])
                            if pend is not None:
                                stage_b(*pend)
                            pend = (t, half, li)
                            yield
                    stage_b(*pend)
                    yield

                cut(4)
                def stream_inproj(h):
                    hb = h % 2

                    def part_g():
                        wb, wr = WS.get(id_g[h])
                        for t in range(8):
                            pb = proj_tok(wb, wr, 512, hT, hres[t], t)
                            ei = t % 2
                            if h > 0:
                                S.op("act", lambda e, pb=pb, t=t: e.activation(out=sg[hb][:, t, :], in_=banks[pb][:], func=AF.Silu),
                                     R=[bres[pb]], W=[sgres[hb][t]] + ([aliasfence] if hb == 1 else []))
                                yield
                                continue
                            S.op("act", lambda e, pb=pb, ei=ei: e.activation(out=etmp[ei][:], in_=banks[pb][:], func=AF.Exp, scale=-1.0),
                                 R=[bres[pb]], W=[etres[ei]])
                            S.op("act", lambda e, ei=ei: e.activation(out=etmp[ei][:], in_=etmp[ei][:], func=AF.Ln, bias=1.0),
                                 R=[etres[ei]], W=[etres[ei]])
                            S.op("act", lambda e, ei=ei: e.activation(out=etmp[ei][:], in_=etmp[ei][:], func=AF.Exp, scale=-1.0),
                                 R=[etres[ei]], W=[etres[ei]])
                            S.op("dve", lambda e, pb=pb, t=t, ei=ei: e.tensor_tensor(out=sg[hb][:, t, :], in0=banks[pb][:], in1=etmp[ei][:], op=ALU.mult),
                                 R=[bres[pb], etres[ei]], W=[sgres[hb][t]] + ([aliasfence] if hb == 1 else []))
                            yield

                    def part_v():
                        wb, wr = WS.get(id_v[h])
                        for t in range(8):
                            pb = proj_tok(wb, wr, 512, hT, hres[t], t)
                            if prev:
                                S.op("act", lambda e, pb=pb, t=t: e.activation(out=vEO[0][:, t, :], in_=banks[pb][:], func=AF.Copy),
                                     R=[bres[pb]], W=[vres[t]])
                                yield
                                continue
                            S.op("act", lambda e, pb=pb, t=t: e.activation(out=vEO[0][0:64, t, :], in_=banks[pb][0:64, :], func=AF.Copy),
                                 R=[bres[pb]], W=[vres[t]])
                            S.op("act", lambda e, pb=pb, t=t: e.activation(out=vEO[1][64:128, t, :], in_=banks[pb][64:128, :], func=AF.Copy),
                                 R=[bres[pb]], W=[vres[t]])
                            yield

                    def part_q():
                        wb, wr = WS.get(id_q[h])
                        for s in range(2):
                            for G in range(2):
                                pb = next_pj()

                                def fn(e, s=s, G=G, pb=pb, wb=wb):
                                    for k in range(16):
                                        ins = e.matmul(banks[pb][:], lhsT=wb[:, k, s * 128:(s + 1) * 128], rhs=hT[:, k, G * 512:(G + 1) * 512],
                                                       start=(k == 0), stop=(k == 15))
                                    return ins
                                S.op("pe", fn, R=[wr] + [r for t in range(4 * G, 4 * G + 4) for r in hres[t]], W=[bres[pb]])
                                pv = banks[pb][:].rearrange("p (t c) -> p t c", t=4)
                                S.op("dve", lambda e, s=s, G=G, pv=pv: e.tensor_scalar(out=qEO[0][:, s, 4 * G:4 * G + 4, 0:64], in0=pv[:, :, 0:64],
                                                                                        scalar1=1.0 / 16.0, scalar2=None, op0=ALU.mult),
                                     R=[bres[pb]], W=[qres[s * 2 + G]])
                                S.op("dve", lambda e, s=s, G=G, pv=pv: e.tensor_scalar(out=qEO[1][:, s, 4 * G:4 * G + 4, 64:128], in0=pv[:, :, 64:128],
                                                                                        scalar1=1.0 / 16.0, scalar2=None, op0=ALU.mult),
                                     R=[bres[pb]], W=[qres[s * 2 + G]])
                                yield


                    if prev:
                        yield from part_v()
                    elif h == 0:
                        yield from part_v()
                        yield from part_q()
                        yield from part_g()
                    else:
                        yield from part_g()
                        yield from part_v()
                        yield from part_q()

                def stream_scan(h):
                    hb = h % 2

                    def o_mm(c):
                        t, hh = c // 2, c % 2
                        ci = c % 2

                        def ofn(e):
                            for s_ in range(2):
                                ins = e.matmul(banks[OBK][:], lhsT=qEO[hh][:, s_, t, :], rhs=Sbf[ci][:, s_, :],
                                               start=(hh == 0 and s_ == 0), stop=(hh == 1 and s_ == 1))
                            return ins
                        S.op("pe", ofn, R=[qres[0 + (c // 8)], qres[2 + (c // 8)], sbres[ci][0], sbres[ci][1]], W=[obres])

                    def opost_stats(t):
                        sl = stat_slot(2)
                        r2 = Res()
                        S.op("act", lambda e: e.activation(out=junk2[:], in_=banks[OBK][:], func=AF.Square, accum_out=stat[:, sl:sl + 1]),
                             R=[obres], W=[junk2res, r2])
                        S.op("pool", lambda e: e.tensor_scalar(out=stat[:, sl + 1:sl + 2], in0=stat[:, sl:sl + 1], scalar1=1.0 / 512.0, scalar2=EPS,
                                                               op0=ALU.mult, op1=ALU.add), R=[r2], W=[r2])
                        S.op("pool", lambda e: e.tensor_tensor(out=stat[:, sl + 1:sl + 2], in0=stat[:, sl + 1:sl + 2], in1=neghalf[:, 0:1], op=ALU.pow),
                             R=[r2, constres], W=[r2])
                        return sl, r2

                    def opost_apply(t, sl, r2):
                        oi = t % 2
                        S.op("dve", lambda e: e.scalar_tensor_tensor(
                            out=onb[oi][:], in0=banks[OBK][:], scalar=stat[:, sl + 1:sl + 2], in1=sg[hb][:, t, :],
                            op0=ALU.mult, op1=ALU.mult), R=[obres, r2, sgres[hb][t]], W=[onres[oi]])

                    def opost_tr(t):
                        oi = t % 2
                        tb = next_tr()

                        def tfn(e):
                            for j in range(4):
                                ins = e.transpose(out=banks[tb][:, j * 128:(j + 1) * 128], in_=onb[oi][:, j * 128:(j + 1) * 128],
                                                  identity=ident[:])
                            return ins
                        S.op("pe", tfn, R=[onres[oi], constres], W=[bres[tb]])
                        return tb

                    def opost_evac(t, tb):
                        S.op("dve", lambda e: e.tensor_tensor(
                            out=oT[:, 4 * h:4 * h + 4, t * 128:(t + 1) * 128],
                            in0=banks[tb][:].rearrange("p (j c) -> p j c", j=4),
                            in1=ogain[:, 4 * h:4 * h + 4].unsqueeze(2).to_broadcast([128, 4, 128]),
                            op=ALU.mult), R=[bres[tb], constres], W=[oTres[t][h]] + kexpres_all)

                    pend_o = None
                    pend_tr = None
                    for c in range(17 if not prev else 8):
                        t, hh = (c // 2, c % 2) if not prev else (c, 0)
                        do_upd = c < 16
                        if do_upd:
                            for s in range(2):
                                ub = (TRB[s] if (prev and c % 2 == 1) else UPB[s])
                                S.op("pe", lambda e, t=t, hh=hh, s=s, ub=ub: e.matmul(
                                    banks[ub][:], lhsT=kdec[:, t, h * 256 + s * 128:h * 256 + (s + 1) * 128],
                                    rhs=vEO[hh][:, t, :], start=True, stop=True),
                                    R=[kdres[t][h // 2], vres[t]], W=[bres[ub]])
                        o_now = pend_o
                        if o_now is not None:
                            o_mm(o_now)
                        tb = opost_tr(pend_tr) if pend_tr is not None else None
                        if do_upd:
                            for s in range(2):
                                ub = (TRB[s] if (prev and c % 2 == 1) else UPB[s])
                                S.op("dve", lambda e, t=t, hh=hh, s=s, ub=ub: e.scalar_tensor_tensor(
                                    out=Sst[:, h, s, :], in0=Sst[:, h, s, :], scalar=decay[:, t, h * 2 + s, hh:hh + 1], in1=banks[ub][:],
                                    op0=ALU.mult, op1=ALU.add), R=[bres[ub], decres[t][h // 2]], W=[sres[h][s]])
                        if tb is not None:
                            opost_evac(pend_tr, tb)
                            pend_tr = None
                        stats = None
                        if o_now is not None and o_now % 2 == 1:
                            stats = opost_stats(o_now // 2)
                        if do_upd and not prev:
                            ci = c % 2
                            for s in range(2):
                                S.op("act", lambda e, s=s, ci=ci: e.activation(out=Sbf[ci][:, s, :], in_=Sst[:, h, s, :], func=AF.Copy),
                                     R=[sres[h][s]], W=[sbres[ci][s]])
                        if stats is not None:
                            opost_apply(o_now // 2, *stats)
                            pend_tr = o_now // 2
                        pend_o = c if (do_upd and not prev) else None
                        yield
                    if pend_tr is not None:
                        tb = opost_tr(pend_tr)
                        opost_evac(pend_tr, tb)
                        yield

                def drain(g):
                    for _ in g:
                        pass

                def merge(a, b, na=1, nb=1):
                    done_a = done_b = False
                    while not (done_a and done_b):
                        for _ in range(nb):
                            if not done_b:
                                try:
                                    next(b)
                                except StopIteration:
                                    done_b = True
                        for _ in range(na):
                            if not done_a:
                                try:
                                    next(a)
                                except StopIteration:
                                    done_a = True

                if carry is not None:
                    S.op("pool", lambda e: e.memset(vEO[1][0:64, :, :], 0.0), W=vres)
                    S.op("pool", lambda e: e.memset(qEO[0][:, :, :, 64:128], 0.0), W=qres)
                    S.op("pool", lambda e: e.memset(qEO[1][:, :, :, 0:64], 0.0), W=qres)
                if hT_ready:
                    glr_group(0)
                    glr_group(1)
                    ga, gb = stream_inproj(0), gating_gen()
                    next(carry, None)
                    next(carry, None)
                    for _ in range(8):
                        next(ga)
                        next(gb)
                        next(carry, None)
                    drain(carry)
                    S.op("pool", lambda e: e.memset(vEO[0][64:128, :, :], 0.0), W=vres)
                    merge(ga, gb, 1, 1)
                pre = prenorm_gen() if not hT_ready else iter(())
                for _ in range(2 if not hT_ready else 0):
                    next(pre)
                    if carry is not None:
                        for _c in range(4):
                            next(carry, None)
                if not hT_ready:
                    if carry is not None:
                        drain(carry)
                        S.op("pool", lambda e: e.memset(vEO[0][64:128, :, :], 0.0), W=vres)
                    ga, gb = stream_inproj(0), gating_gen()
                    for _ in range(2):
                        next(ga)
                        next(pre)
                    glr_group(0)
                    for _ in range(4):
                        next(gb)
                        next(gb)
                        next(ga)
                        next(pre)
                    glr_group(1)
                    merge(ga, gb, 1, 2 if prev else 1)
                for n in range(2):
                    wb, wr = WS.get(id_k[n])
                    for t in range(8):
                        pb = proj_tok(wb, wr, 512, hT, hres[t], t)
                        S.op("dve", lambda e, pb=pb, t=t, n=n: e.tensor_tensor(out=kdec[:, t, n * 512:(n + 1) * 512], in0=banks[pb][:],
                                                                                in1=kexp[:, t, n * 512:(n + 1) * 512], op=ALU.mult),
                             R=[bres[pb], kexpres[t][n]], W=[kdres[t][n]])


                cut(41)
                if CUT[0] == 42:
                    drain(stream_scan(0))
                    cut(42)
                for h in range(4):
                    if prev and h + 1 == 3:
                        a3, b3 = stream_inproj(3), stream_scan(2)
                        own_pre = prenorm_gen(x_d)
                        for _ in range(8):
                            next(b3, None)
                            next(a3, None)
                            next(own_pre, None)
                        for _ in b3:
                            pass
                        for _ in a3:
                            pass
                        for _ in own_pre:
                            pass
                    elif h + 1 < 4:
                        merge(stream_inproj(h + 1), stream_scan(h), 1, 1)
                    elif prev:
                        return stream_scan(h)
                    else:
                        drain(stream_scan(h))

            obres = bres[OBK]
            DBG.update(oT=oT.name, kdec=kdec.name, vE=vEO[0].name, vO=vEO[1].name, qE=qEO[0].name, qO=qEO[1].name, Sst=Sst.name, decay=decay.name, gaug=gaug.name, hT=hT.name)
            junk2 = T([128, 512], BF16, 2304)
            junk2res = Res()
            kexpres_all = [r for t in range(8) for r in kexpres[t]]

            cut(1)
            carry0 = gla_phase(True)
            cut(5)
            gla_phase(False, carry0, hT_ready=True)
            cut(6)

            ybuf = [x1[:, t, :] for t in range(8)]
            ybres = x1res
            nchunk = [T([128, 512], F32, A + 0 + i * 2 * KB) for i in range(2)]
            ncres = [Res(), Res()]
            XR = [T([128, 2048], F32, A + 8 * KB + i * 8 * KB) for i in range(2)]
            xrres = [Res(), Res()]
            junk3 = T([128, 512], BF16, A + 24 * KB)
            junk3res = Res()
            lastrecs = [S.q[e][-1] for e in ENGS if S.q[e]]
            fence0 = S.op("dve", lambda e: e.memset(stat[:, 0:1], 0.0), deps=lastrecs)
            for r in ybres + ncres + xrres + [junk3res] + x1res:
                r.w = fence0

            def finish0(t, tt, rstd_ap, rres):
                i = t % 2
                S.op("sp", lambda e: e.dma_start(out=XR[i][:], in_=x_d[t * 128:(t + 1) * 128, :]), W=[xrres[i]], key="xr%d" % i)
                S.op("dve", lambda e: e.scalar_tensor_tensor(out=x1[:, t, :], in0=x1[:, t, :], scalar=rstd_ap, in1=XR[i][:],
                                                             op0=ALU.mult, op1=ALU.add),
                     R=[ybres[tt], rres, xrres[i]], W=[x1res[t]])
                if mode == "l0":
                    S.op("sp", lambda e: e.dma_start(out=out_d[t * 128:(t + 1) * 128, :], in_=x1[:, t, :]), R=[x1res[t]], key="o%d" % t)

            outproj(0, lambda k, t: oT[:, k, t * 128:(t + 1) * 128], lambda t: oTres[t], out_ids0, ybuf, ybres, nchunk, ncres,
                    junk3, junk3res, finish0, ngrp=1)

        if do1:
            lastrecs = [S.q[e][-1] for e in ENGS if S.q[e]]
            fence1 = S.op("dve", lambda e: e.memset(stat[:, 0:1], 0.0), deps=lastrecs)
            hT1 = T([128, 16, 1024], BF16, A + 0)
            P = T([128, 8, 4, 512], F32, A + 32 * KB)
            Pbf = P.bitcast(BF16) if hasattr(P, "bitcast") else None
            tmpA = [T([128, 512], F32, A + 96 * KB + i * 2 * KB) for i in range(2)]
            XT1 = [T([128, 2048], F32, A + 32 * KB + (6 + i) * 8 * KB) for i in range(2)]
            junk1 = T([128, 2048], BF16, A + 96 * KB)
            hres1 = [[Res() for _ in range(4)] for _ in range(8)]
            xt1res = [Res(), Res()]
            junk1res = Res()
            pres = [[Res() for _ in range(4)] for _ in range(8)]
            tmpres = [Res(), Res()]
            for r in [x for hs in hres1 for x in hs] + xt1res + [junk1res] + tmpres:
                r.w = fence1
            if mode == "l1":
                for t in range(8):
                    S.op("sp", lambda e, t=t: e.dma_start(out=x1[:, t, :], in_=x_d[t * 128:(t + 1) * 128, :]), W=[x1res[t]], key="x1l%d" % t)
            identb = T([128, 128], BF16, 512)
            identres = Res()
            S.op("dve", lambda e: e.tensor_copy(out=identb[:], in_=ident[:]), R=[constres], W=[identres], deps=[fence1])
            wsres, bspres = Res(), Res()
            wsres.w = fence1
            S.op("pool", lambda e: e.dma_start(out=wsT[:].rearrange("p g i -> p (g i)"), in_=wsT_d[:, :]), W=[wsres], key="wsT")
            S.op("dve", lambda e: e.memset(wsT[64:128, :, 0:64], 0.0), W=[wsres])
            S.op("sp", lambda e: e.dma_start(out=bsp[:], in_=bsp_d[:, :]), W=[bspres], key="bsp")

            if not do0:
                l1_ids = l1_plan()
            id_v1, id_u1, id_g1, out_ids1 = l1_ids

            for t in range(8):
                for n in range(4):
                    pres[t][n].w = fence1
            bnres = [Res() for _ in range(8)]

            def pre1_part(t, part):
                prenorm_tile(1, t, x1[:, t, :], x1res[t], False, XT1, xt1res, junk1, [junk1res] + tmpres, hT1, hres1, t % 2, pool_rstd=True,
                             part=part)

            _p1 = {"started": False}

            def pre1(t):
                if not _p1["started"]:
                    pre1_part(0, "front")
                    _p1["started"] = True
                pre1_part(t, "back")
                if t + 1 < 8:
                    pre1_part(t + 1, "front")

            def v_mm(n, t, wb, wr):
                return proj_tok(wb, wr, 512, hT1, hres1[t], t)

            def v_ev(n, t, pb):
                S.op("act", lambda e: e.activation(out=P[:, t, n, :], in_=banks[pb][:], func=AF.Gelu_apprx_tanh),
                     R=[bres[pb]], W=[pres[t][n]])
                S.op("dve", lambda e: e.bn_stats(out=bnst[:, t, n, :], in_=P[:, t, n, :]), R=[pres[t][n]], W=[bnres[t]])

            def v_unit(n, t, wb, wr):
                v_ev(n, t, v_mm(n, t, wb, wr))

            lnres_t = [Res() for _ in range(8)]
            gch = tmpA[1]
            bch = T([128, 512], F32, A + 164 * KB)
            bchres = Res()
            bchres.w = fence1
            utmp = tmpA[0]

            def load_gb(n):
                S.op("sp", lambda e: e.dma_start(out=gch[:], in_=lng_d[:, n * 512:(n + 1) * 512]), W=[tmpres[1]], key="lng")
                S.op("sp", lambda e: e.dma_start(out=bch[:], in_=lnb_d[:, n * 512:(n + 1) * 512]), W=[bchres], key="lnb")

            def ln_unit(n, t):
                S.op("act", lambda e: e.activation(out=P[:, t, n, :], in_=P[:, t, n, :], func=AF.Identity,
                                                   scale=lnsc[:, t, 0:1], bias=lnsc[:, t, 1:2]),
                     R=[lnres_t[t]], W=[pres[t][n]])
                S.op("dve", lambda e: e.tensor_tensor(out=P[:, t, n, :], in0=P[:, t, n, :], in1=gch[:], op=ALU.mult),
                     R=[tmpres[1]], W=[pres[t][n]])
                S.op("dve", lambda e: e.tensor_tensor(out=Pbf[:, t, n, 0:512], in0=P[:, t, n, :], in1=bch[:], op=ALU.add),
                     R=[bchres], W=[pres[t][n]])

            def sp_unit(n, t):
                ub = UPB[n % 2]

                def sfn(e):
                    for gg in range(2):
                        g = n * 2 + gg
                        ins = e.matmul(banks[ub][:, gg * 256:(gg + 1) * 256], lhsT=wsT[:, g, :],
                                       rhs=Pbf[:, t, n, gg * 256:(gg + 1) * 256], start=True, stop=True)
                    return ins
                S.op("pe", sfn, R=[pres[t][n], wsres], W=[bres[ub]])
                for gg in range(2):
                    g = n * 2 + gg
                    if n % 2 == 0:
                        S.op("dve", lambda e, gg=gg, g=g: e.tensor_scalar(
                            out=P[:, t, n, gg * 256:(gg + 1) * 256], in0=banks[ub][:, gg * 256:(gg + 1) * 256],
                            scalar1=bsp[:, g:g + 1], scalar2=None, op0=ALU.add), R=[bres[ub], bspres], W=[pres[t][n]])
                    else:
                        S.op("act", lambda e, gg=gg, g=g: e.activation(
                            out=P[:, t, n, gg * 256:(gg + 1) * 256], in_=banks[ub][:, gg * 256:(gg + 1) * 256],
                            func=AF.Identity, bias=bsp[:, g:g + 1]), R=[bres[ub], bspres], W=[pres[t][n]])

            def u_mm(n, t, wb, wr):
                return proj_tok(wb, wr, 512, hT1, hres1[t], t)

            def u_ev(n, t, pb):
                S.op("act", lambda e: e.activation(out=utmp[:], in_=banks[pb][:], func=AF.Gelu_apprx_tanh),
                     R=[bres[pb]], W=[tmpres[0]])
                S.op("dve", lambda e: e.tensor_tensor(out=P[:, t, n, :], in0=P[:, t, n, :], in1=utmp[:], op=ALU.mult),
                     R=[tmpres[0]], W=[pres[t][n]])

            def ln_stats(t):
                S.op("dve", lambda e: e.bn_aggr(out=mv[:, t, :], in_=bnst[:, t, :, :].rearrange("p a b -> p (a b)")),
                     R=[bnres[t]], W=[lnres_t[t]])
                S.op("pool", lambda e: e.tensor_scalar(out=lnsc[:, t, 0:1], in0=mv[:, t, 1:2], scalar1=EPS, scalar2=None, op0=ALU.add),
                     R=[lnres_t[t]], W=[lnres_t[t]])
                S.op("pool", lambda e: e.tensor_tensor(out=lnsc[:, t, 0:1], in0=lnsc[:, t, 0:1], in1=neghalf[:, 0:1], op=ALU.pow),
                     R=[lnres_t[t], constres], W=[lnres_t[t]])
                S.op("dve", lambda e: e.scalar_tensor_tensor(out=lnsc[:, t, 1:2], in0=mv[:, t, 0:1], scalar=-1.0, in1=lnsc[:, t, 0:1],
                                                             op0=ALU.mult, op1=ALU.mult), R=[lnres_t[t]], W=[lnres_t[t]])

            pre1(0)
            pre1(1)
            for n in range(4):
                wb, wr = WS.get(id_v1[n])
                if n == 3:
                    load_gb(0)
                for t in range(8):
                    if n == 3:
                        pb3 = v_mm(n, t, wb, wr)
                        if t > 0:
                            ln_stats(t - 1)
                            ln_unit(0, t - 1)
                        if t > 1:
                            sp_unit(0, t - 2)
                        v_ev(n, t, pb3)
                        continue
                    v_unit(n, t, wb, wr)
                    if n == 0 and t + 2 < 8:
                        pre1(t + 2)
                    if n == 0 and t == 5:
                        pf = S.op("dve", lambda e: e.memset(stat[:, 0:1], 0.0), deps=[S.q[e_][-1] for e_ in ENGS if S.q[e_]])
                        for tt_ in (6, 7):
                            for nn_ in range(4):
                                pres[tt_][nn_].w = pf
            ln_stats(7)
            ln_unit(0, 7)
            sp_unit(0, 6)
            sp_unit(0, 7)
            cut(24)
            for n in range(4):
                wb, wr = WS.get(id_u1[n])
                if n + 1 < 4:
                    load_gb(n + 1)
                for t in range(8):
                    pbu = u_mm(n, t, wb, wr)
                    if n + 1 < 4:
                        ln_unit(n + 1, t)
                        if t > 0:
                            sp_unit(n + 1, t - 1)
                    u_ev(n, t, pbu)
                if n + 1 < 4:
                    sp_unit(n + 1, 7)
            cut(25)
            def g_tail(t, n):
                tb = next_tr()

                def tfn(e):
                    for j in range(4):
                        ins = e.matmul(banks[tb][:, j * 128:(j + 1) * 128], lhsT=Pbf[:, t, n, j * 128:(j + 1) * 128], rhs=identb[:],
                                       start=True, stop=True)
                    return ins
                S.op("pe", tfn, R=[pres[t][n], identres], W=[bres[tb]])
                S.op("act", lambda e: e.activation(out=Pbf[:, t, n, 512:1024], in_=banks[tb][:], func=AF.Copy),
                     R=[bres[tb]], W=[pres[t][n]])

            pend = None
            for n in range(4):
                wb, wr = WS.get(id_g1[n])
                for t in range(8):
                    pb = proj_tok(wb, wr, 512, hT1, hres1[t], t)
                    if pend is not None:
                        g_tail(*pend)
                    ti = t % 2
                    S.op("act", lambda e, pb=pb, ti=ti: e.activation(out=tmpA[ti][:], in_=banks[pb][:], func=AF.Silu),
                         R=[bres[pb]], W=[tmpres[ti]])
                    S.op("dve", lambda e, t=t, n=n, ti=ti: e.tensor_tensor(out=Pbf[:, t, n, 0:512], in0=P[:, t, n, :], in1=tmpA[ti][:], op=ALU.mult),
                         R=[tmpres[ti]], W=[pres[t][n]])
                    pend = (t, n)
            g_tail(*pend)
            cut(26)
            lastrecs = [S.q[e][-1] for e in ENGS if S.q[e]]
            fence2 = S.op("dve", lambda e: e.memset(stat[:, 0:1], 0.0), deps=lastrecs)
            ybuf1 = [T([128, 2048], F32, A + 0 + i * 8 * KB) for i in range(4)]
            ybres1 = [Res() for _ in range(8)]
            ncres1 = [Res(), Res()]
            junk4 = T([128, 512], BF16, 2304)
            junk4res = Res()
            for r in ybres1 + ncres1 + [junk4res]:
                r.w = fence2
            outs = []

            def yfrag(t):
                return P[:, 2 * (t - 4):2 * (t - 4) + 2, :, 0:256]

            def yslice1(tt, n):
                if tt < 4:
                    return ybuf1[tt][:, n * 512:(n + 1) * 512], (lambda ap: ap)
                yo = P[:, 2 * (tt - 4) + n // 2, (n % 2) * 2:(n % 2) * 2 + 2, 0:256]
                return yo, (lambda ap: ap.rearrange("p (a c) -> p a c", a=2))

            def finish1(t, tt, rstd_ap, rres):
                if t < 4:
                    yt, xv, ov = ybuf1[t][:], x1[:, t, :], out_d[t * 128:(t + 1) * 128, :]
                else:
                    yt = yfrag(t)
                    xv = x1[:, t, :].rearrange("p (a b c) -> p a b c", a=2, b=4)
                    ov = out_d[t * 128:(t + 1) * 128, :].rearrange("p (a b c) -> p a b c", a=2, b=4)
                S.op("dve", lambda e: e.scalar_tensor_tensor(out=yt, in0=yt, scalar=rstd_ap, in1=xv, op0=ALU.mult, op1=ALU.add),
                     R=[rres, x1res[t]], W=[ybres1[tt]])
                outs.append(S.op("sp", lambda e: e.dma_start(out=ov, in_=yt), R=[ybres1[tt]], key="o%d" % t))

            outproj(1, lambda k, t: Pbf[:, t, k // 4, 512 + (k % 4) * 128:512 + (k % 4 + 1) * 128], lambda t: pres[t], out_ids1, ybuf1, ybres1,
                    tmpA, tmpres, junk4, junk4res, finish1, ngrp=1, yslice=yslice1)


    except _Cut:
        pass
    for eng in ("sp",):
        tails = [r for r in S.q["sp"] if r.dma and r.semkey[1].startswith("o")]
        S.op("sp", None, deps=tails)
    S.emit()
    es.close()
    return nc


_CACHE = {}


def _get(mode):
    if mode not in _CACHE:
        _CACHE[mode] = build(mode)
    return _CACHE[mode]


def _consts():
    ident = np.eye(128, dtype=np.float32)
    s = np.arange(128)[:, None]
    t = np.arange(128)[None, :]
    umat = np.where((s > t) & (s // 64 == t // 64), -1.0 / 16.0, 0.0).astype(np.float32)
    ind = np.zeros((128, 2), np.float32)
    ind[:64, 0] = -1.0 / 16.0
    ind[64:, 1] = -1.0 / 16.0
    umatp = np.where(s > t, -1.0 / 16.0, 0.0).astype(np.float32)
    indp = np.zeros((128, 2), np.float32)
    indp[:, 0] = -1.0 / 16.0
    return ident, umat, ind, umatp, indp


def _fm(v):
    return np.ascontiguousarray(np.asarray(v, np.float32).reshape(16, 128).T)


FUSED = True


def kernel(x, norm_pre, norm_post, gla_w_in, gla_w_gate2, gla_b_gate, gla_o_gain, gla_w_out,
           sgu_w_in, sgu_ln_gain, sgu_ln_bias, sgu_w_spatial, sgu_b_spatial, sgu_w_out):
    f = lambda a: np.ascontiguousarray(np.asarray(a, dtype=np.float32))
    x = f(x)
    ident, umat, ind, umatp, indp = _consts()
    gpre = np.concatenate([_fm(norm_pre[0]), _fm(norm_pre[1])], axis=1)
    npost = np.ascontiguousarray(np.broadcast_to(f(norm_post)[:, None, :], (2, 128, 2048)))
    waug = np.concatenate([f(gla_w_gate2)[0], f(gla_b_gate)[0][None, :]], axis=0)
    common = {"ident": ident, "gpre": gpre, "npost": npost}
    l0 = {"win0": f(gla_w_in)[0], "waug": waug, "wout0": f(gla_w_out)[0], "ogain": _fm(gla_o_gain[0]),
          "umat": umat, "ind": ind, "umatp": umatp, "indp": indp}
    wsT = np.ascontiguousarray(np.transpose(f(sgu_w_spatial)[0], (2, 0, 1)).reshape(128, 1024))
    l1 = {"win1": f(sgu_w_in)[0], "wout1": f(sgu_w_out)[0],
          "lng": np.ascontiguousarray(np.broadcast_to(f(sgu_ln_gain)[0][None, :], (128, 2048))),
          "lnb": np.ascontiguousarray(np.broadcast_to(f(sgu_ln_bias)[0][None, :], (128, 2048))),
          "wsT": wsT, "bsp": np.ascontiguousarray(f(sgu_b_spatial)[0].T)}
    zeros = np.zeros((1024, 2048), np.float32)
    cores = list(range(8))

    def shard(xx):
        return [np.ascontiguousarray(xx[c // 2, (c % 2) * 1024:(c % 2 + 1) * 1024]) for c in cores]

    def prevs(xx):
        return [np.ascontiguousarray(xx[c // 2, 0:1024]) if c % 2 == 1 else zeros for c in cores]

    xs, xps = shard(x), prevs(x)
    if FUSED:
        nc = _get("fused")
        maps = [dict(common, **l0, **l1, x=xs[c], xp=xps[c]) for c in cores]
        res = run_bass_kernel_spmd(nc, maps, core_ids=cores)
        outs = [r["out"] for r in res.results]
    else:
        nc0 = _get("l0")
        maps = [dict(common, **l0, x=xs[c], xp=xps[c]) for c in cores]
        res = run_bass_kernel_spmd(nc0, maps, core_ids=cores)
        x1s = [r["out"] for r in res.results]
        nc1 = _get("l1")
        maps = [dict(common, **l1, x=x1s[c]) for c in cores]
        res = run_bass_kernel_spmd(nc1, maps, core_ids=cores)
        outs = [r["out"] for r in res.results]
    out = np.empty((4, 2048, 2048), np.float32)
    for c in cores:
        out[c // 2, (c % 2) * 1024:(c % 2 + 1) * 1024] = outs[c]
    return out
```
